# Optimizing a Trainium2 kernel written in Bass

```python
import math
import jax, jax.numpy as jnp
from jax import lax
import numpy as np

D_MODEL = 1024
BATCH = 8
SEQ = 2048
DEPTH = 4
DEC_BATCH = 128
DEC_SEQ = 1
PAST_LEN = 16384
PAGE_SIZE = 128

N_META = 16
D_CONV = D_MODEL
CONV_A_WIDTH = 3
CONV_A_GROUPS = 16
D_RNN = D_MODEL
RG_HEADS = 16
RG_HEAD_DIM = D_RNN // RG_HEADS
CONV_B_WIDTH = 4
RG_C = 8.0
D_FF = 2816
FFN_CONV_WIDTH = 3
EPS = 1e-6
D_IN = 3 * D_CONV + 2 * D_RNN + 2 * D_MODEL

kernel_name = "hybrid_shortconv_rglru_convffn_meta_step"


def _rmsnorm(x, g):
    xf = x.astype(jnp.float32)
    y = xf * lax.rsqrt(jnp.mean(xf * xf, axis=-1, keepdims=True) + EPS)
    return (y * g.astype(jnp.float32)).astype(x.dtype)


def _causal_dwconv(x, buf, w):
    width = w.shape[0]
    t = x.shape[1]
    xp = jnp.concatenate([buf.astype(x.dtype), x], axis=1)
    y = xp[:, 0:t] * w[0]
    for k in range(1, width):
        y = y + xp[:, k:k + t] * w[k]
    return y, xp[:, xp.shape[1] - (width - 1):]


def _rg_lru(xc, h0, w_a, b_a, w_x, b_x, lam, is_first):
    bn, t, c = xc.shape
    xh = xc.reshape(bn, t, RG_HEADS, RG_HEAD_DIM)
    r = jax.nn.sigmoid(jnp.einsum('bthi,hij->bthj', xh, w_a).reshape(bn, t, c) + b_a)
    i = jax.nn.sigmoid(jnp.einsum('bthi,hij->bthj', xh, w_x).reshape(bn, t, c) + b_x)
    log_a = -RG_C * r.astype(jnp.float32) * jax.nn.softplus(-lam.astype(jnp.float32))
    a = jnp.exp(log_a)
    mult = jnp.sqrt(-jnp.expm1(2.0 * log_a))
    mult = jnp.where(is_first[None, :, None], 1.0, mult)
    b = mult * (i * xc).astype(jnp.float32)

    def step(h, ab):
        h = ab[0] * h + ab[1]
        return h, h

    h_last, hs = lax.scan(step, h0.astype(jnp.float32),
                          (a.transpose(1, 0, 2), b.transpose(1, 0, 2)))
    return hs.transpose(1, 0, 2).astype(xc.dtype), h_last


def _layer(x, buf_a, buf_b, h0, buf_f, is_first, norm_mix, norm_ffn, w_in, b_gate,
           conv_a_w, w_a_out, conv_b_w, conv_b_b, rg_w_a, rg_b_a, rg_w_x, rg_b_x,
           rg_lambda, w_b_out, w_o, ffn_w_up, ffn_w_gate, ffn_conv_w, ffn_conv_b, ffn_w_down):
    hn = _rmsnorm(x, norm_mix)
    proj = hn @ w_in
    o = 0
    gB = proj[..., o:o + D_CONV]; o += D_CONV
    gC = proj[..., o:o + D_CONV]; o += D_CONV
    ha = proj[..., o:o + D_CONV]; o += D_CONV
    xr = proj[..., o:o + D_RNN]; o += D_RNN
    gr = proj[..., o:o + D_RNN]; o += D_RNN
    merge = proj[..., o:o + 2 * D_MODEL] + b_gate
    conv_a, nbuf_a = _causal_dwconv(gC * ha, buf_a, conv_a_w)
    y_a = (gB * conv_a) @ w_a_out
    xc, nbuf_b = _causal_dwconv(xr, buf_b, conv_b_w)
    xc = xc + conv_b_b
    hs, h_last = _rg_lru(xc, h0, rg_w_a, rg_b_a, rg_w_x, rg_b_x, rg_lambda, is_first)
    y_b = (jax.nn.gelu(gr, approximate=True) * hs) @ w_b_out
    mixed = (jax.nn.sigmoid(merge[..., :D_MODEL]) * y_a
             + jax.nn.sigmoid(merge[..., D_MODEL:]) * y_b)
    x = x + mixed @ w_o
    hf = _rmsnorm(x, norm_ffn)
    u = hf @ ffn_w_up
    uc, nbuf_f = _causal_dwconv(u, buf_f, ffn_conv_w)
    uc = uc + ffn_conv_b
    x = x + (jax.nn.silu(uc) * (hf @ ffn_w_gate)) @ ffn_w_down
    return x, nbuf_a, nbuf_b, h_last.astype(x.dtype), nbuf_f


def _trunk(x, start_pos, bufs_a, bufs_b, hs0, bufs_f, norm_mix, norm_ffn, norm_final, w_in,
           b_gate, conv_a_w, w_a_out, conv_b_w, conv_b_b, rg_w_a, rg_b_a, rg_w_x, rg_b_x,
           rg_lambda, w_b_out, w_o, ffn_w_up, ffn_w_gate, ffn_conv_w, ffn_conv_b, ffn_w_down):
    t = x.shape[1]
    is_first = (start_pos + jnp.arange(t)) == 0
    na, nb, nh, nf = [], [], [], []
    for l in range(DEPTH):
        x, a_, b_, h_, f_ = _layer(
            x, bufs_a[l], bufs_b[l], hs0[l], bufs_f[l], is_first, norm_mix[l], norm_ffn[l],
            w_in[l], b_gate[l], conv_a_w[l], w_a_out[l], conv_b_w[l], conv_b_b[l], rg_w_a[l],
            rg_b_a[l], rg_w_x[l], rg_b_x[l], rg_lambda[l], w_b_out[l], w_o[l], ffn_w_up[l],
            ffn_w_gate[l], ffn_conv_w[l], ffn_conv_b[l], ffn_w_down[l])
        na.append(a_); nb.append(b_); nh.append(h_); nf.append(f_)
    y = _rmsnorm(x, norm_final)
    return y, jnp.stack(na), jnp.stack(nb), jnp.stack(nh), jnp.stack(nf)


def setup_inputs(seed: int = 0) -> dict:
    key = jax.random.key(seed)
    ks = jax.random.split(key, 40)
    f32 = jnp.float32
    nrm = lambda k, shape, s: jax.random.normal(k, shape, f32) * s
    u = jax.random.uniform(ks[30], (DEPTH, D_RNN), f32, 0.9, 0.999)
    s = u ** (1.0 / RG_C)
    rg_lambda = jnp.log(s) - jnp.log1p(-s)
    return {
        "x_prompt": nrm(ks[0], (BATCH, SEQ, D_MODEL), 1.0),
        "x_sample": nrm(ks[1], (DEC_BATCH, DEC_SEQ, D_MODEL), 1.0),
        "state_conv_a": nrm(ks[2], (DEPTH, DEC_BATCH, CONV_A_WIDTH - 1, D_CONV), 1.0),
        "state_conv_b": nrm(ks[3], (DEPTH, DEC_BATCH, CONV_B_WIDTH - 1, D_RNN), 1.0),
        "state_rglru": nrm(ks[4], (DEPTH, DEC_BATCH, D_RNN), 0.5),
        "state_conv_ffn": nrm(ks[5], (DEPTH, DEC_BATCH, FFN_CONV_WIDTH - 1, D_FF), 1.0),
        "meta_tokens": nrm(ks[6], (N_META, D_MODEL), 1.0),
        "norm_mix": 1.0 + nrm(ks[7], (DEPTH, D_MODEL), 0.02),
        "norm_ffn": 1.0 + nrm(ks[8], (DEPTH, D_MODEL), 0.02),
        "norm_final": 1.0 + nrm(ks[9], (D_MODEL,), 0.02),
        "w_in": nrm(ks[10], (DEPTH, D_MODEL, D_IN), D_MODEL ** -0.5),
        "b_gate": nrm(ks[11], (DEPTH, 2 * D_MODEL), 0.02),
        "conv_a_w": nrm(ks[12], (DEPTH, CONV_A_WIDTH, D_CONV), CONV_A_WIDTH ** -0.5),
        "w_a_out": nrm(ks[13], (DEPTH, D_CONV, D_MODEL), D_CONV ** -0.5),
        "conv_b_w": nrm(ks[14], (DEPTH, CONV_B_WIDTH, D_RNN), CONV_B_WIDTH ** -0.5),
        "conv_b_b": nrm(ks[15], (DEPTH, D_RNN), 0.02),
        "rg_w_a": nrm(ks[16], (DEPTH, RG_HEADS, RG_HEAD_DIM, RG_HEAD_DIM), RG_HEAD_DIM ** -0.5),
        "rg_b_a": nrm(ks[17], (DEPTH, D_RNN), 0.02),
        "rg_w_x": nrm(ks[18], (DEPTH, RG_HEADS, RG_HEAD_DIM, RG_HEAD_DIM), RG_HEAD_DIM ** -0.5),
        "rg_b_x": nrm(ks[19], (DEPTH, D_RNN), 0.02),
        "rg_lambda": rg_lambda,
        "w_b_out": nrm(ks[20], (DEPTH, D_RNN, D_MODEL), D_RNN ** -0.5),
        "w_o": nrm(ks[21], (DEPTH, D_MODEL, D_MODEL), D_MODEL ** -0.5),
        "ffn_w_up": nrm(ks[22], (DEPTH, D_MODEL, D_FF), D_MODEL ** -0.5),
        "ffn_w_gate": nrm(ks[23], (DEPTH, D_MODEL, D_FF), D_MODEL ** -0.5),
        "ffn_conv_w": nrm(ks[24], (DEPTH, FFN_CONV_WIDTH, D_FF), FFN_CONV_WIDTH ** -0.5),
        "ffn_conv_b": nrm(ks[25], (DEPTH, D_FF), 0.02),
        "ffn_w_down": nrm(ks[26], (DEPTH, D_FF, D_MODEL), D_FF ** -0.5),
    }


def reference(x_prompt, x_sample, state_conv_a, state_conv_b, state_rglru, state_conv_ffn,
              meta_tokens, norm_mix, norm_ffn, norm_final, w_in, b_gate, conv_a_w, w_a_out,
              conv_b_w, conv_b_b, rg_w_a, rg_b_a, rg_w_x, rg_b_x, rg_lambda, w_b_out, w_o,
              ffn_w_up, ffn_w_gate, ffn_conv_w, ffn_conv_b, ffn_w_down):
    weights = (norm_mix, norm_ffn, norm_final, w_in, b_gate, conv_a_w, w_a_out, conv_b_w,
               conv_b_b, rg_w_a, rg_b_a, rg_w_x, rg_b_x, rg_lambda, w_b_out, w_o, ffn_w_up,
               ffn_w_gate, ffn_conv_w, ffn_conv_b, ffn_w_down)
    bp = x_prompt.shape[0]
    dt = x_prompt.dtype
    xp = jnp.concatenate(
        [jnp.broadcast_to(meta_tokens.astype(dt)[None], (bp, N_META, D_MODEL)), x_prompt], axis=1)
    z_a = jnp.zeros((DEPTH, bp, CONV_A_WIDTH - 1, D_CONV), dt)
    z_b = jnp.zeros((DEPTH, bp, CONV_B_WIDTH - 1, D_RNN), dt)
    z_h = jnp.zeros((DEPTH, bp, D_RNN), dt)
    z_f = jnp.zeros((DEPTH, bp, FFN_CONV_WIDTH - 1, D_FF), dt)
    yp, p_conv_a, p_conv_b, p_rglru, p_conv_ffn = _trunk(xp, 0, z_a, z_b, z_h, z_f, *weights)
    y_prompt = yp[:, N_META:]
    y_sample, s_conv_a, s_conv_b, s_rglru, s_conv_ffn = _trunk(
        x_sample, PAST_LEN, state_conv_a, state_conv_b, state_rglru, state_conv_ffn, *weights)
    return (y_prompt, y_sample, p_conv_a, p_conv_b, p_rglru, p_conv_ffn,
            s_conv_a, s_conv_b, s_rglru, s_conv_ffn)
```

```python
from contextlib import ExitStack
import types
import numpy as np
import concourse.bass as bass
import concourse.mybir as mybir
from concourse.bass_utils import run_bass_kernel_spmd

F32 = mybir.dt.float32
BF16 = mybir.dt.bfloat16
AF = mybir.ActivationFunctionType
ALU = mybir.AluOpType

L = 4
D = 1024
DC = 8
DFF = 2816
FC = 22
DIN = 7168
NMETA = 16
SEQ = 2048
NS = 16
NTOK = NMETA + SEQ + NS
NT = 352
NTILES = 6
NPL = 304
SE = NPL + NS
NCOL = NTILES * NT
EPS = 1e-6
GROUPS = [[0, 1], [2, 3], [4, 5]]
GT = max(len(g) for g in GROUPS) * NT
NV = 208
SLOT = 5120
NSLOT = 3

ENGS = ("pe", "act", "dve", "pool", "sp")


def _freeze(fn):
    if fn is None or fn.__closure__ is None:
        return fn
    cells = []
    for c in fn.__closure__:
        try:
            cells.append(types.CellType(c.cell_contents))
        except ValueError:
            cells.append(c)
    return types.FunctionType(fn.__code__, fn.__globals__, fn.__name__, fn.__defaults__, tuple(cells))


class Sched:
    def __init__(self, nc, stack, same_engine_sync=True):
        self.nc = nc
        self.stack = stack
        self.same = same_engine_sync
        self.ops = {e: [] for e in ENGS}
        self.cnt = {}
        self.sems = {}
        self.seen = {e: {} for e in ENGS}
        self.lastw = {}
        self.readers = {}
        for e in ENGS:
            self._sem("E_" + e)

    def _sem(self, key):
        if key not in self.sems:
            self.sems[key] = self.stack.enter_context(self.nc.semaphore(key))
            self.cnt[key] = 0
        return self.sems[key]

    def _deps(self, reads, writes):
        deps = {}

        def add(tok):
            if tok is None:
                return
            k, v = tok
            if deps.get(k, 0) < v:
                deps[k] = v

        for r in reads:
            add(self.lastw.get(r))
        for w in writes:
            add(self.lastw.get(w))
            for t in self.readers.get(w, ()):
                add(t)
        return deps

    def _waits(self, eng, deps):
        waits = []
        seen = self.seen[eng]
        own = "E_" + eng
        for k, v in deps.items():
            if k == own and not self.same:
                continue
            if seen.get(k, 0) < v:
                seen[k] = v
                waits.append((k, v))
        return waits

    def _commit(self, tok, reads, writes):
        for r in reads:
            self.readers.setdefault(r, []).append(tok)
        for w in writes:
            self.lastw[w] = tok
            self.readers[w] = []

    def op(self, eng, fn, reads=(), writes=()):
        deps = self._deps(reads, writes)
        waits = self._waits(eng, deps)
        key = "E_" + eng
        self.cnt[key] += 1
        tok = (key, self.cnt[key])
        self.ops[eng].append((waits, _freeze(fn), key, 1))
        self._commit(tok, reads, writes)
        return tok

    def dma(self, eng, fn, semname, reads=(), writes=()):
        key = "D_" + semname
        self._sem(key)
        deps = self._deps(reads, writes)
        waits = self._waits(eng, deps)
        self.cnt[key] += 16
        tok = (key, self.cnt[key])
        self.ops[eng].append((waits, _freeze(fn), key, 16))
        self._commit(tok, reads, writes)
        return tok

    def final_wait(self, eng, toks):
        deps = {}
        for k, v in toks:
            if deps.get(k, 0) < v:
                deps[k] = v
        waits = self._waits(eng, deps)
        self.ops[eng].append((waits, None, None, 0))

    def emit(self):
        nc = self.nc
        S = self
        with nc.Block() as block:
            def run(e, h):
                for waits, fn, key, inc in S.ops[e]:
                    for k, v in waits:
                        h.wait_ge(S.sems[k], v)
                    if fn is not None:
                        ins = fn(h)
                        ins.then_inc(S.sems[key], inc)

            @block.tensor
            def _(h):
                run("pe", h)

            @block.scalar
            def _(h):
                run("act", h)

            @block.vector
            def _(h):
                run("dve", h)

            @block.gpsimd
            def _(h):
                run("pool", h)

            @block.sync
            def _(h):
                run("sp", h)


V_G1, V_G2, V_BG, V_CAW, V_CBW, V_CBB, V_RBA, V_RBX, V_LAM, V_FCW, V_FCB = (
    0, 8, 16, 32, 56, 88, 96, 104, 112, 120, 186)


def build_nc(debug=False):
    nc = bass.Bass("TRN2", target_bir_lowering=False)

    def din(name, shape):
        return nc.dram_tensor(name, list(shape), F32, kind="ExternalInput").ap()

    def dout(name, shape):
        return nc.dram_tensor(name, list(shape), F32, kind="ExternalOutput").ap()

    xT = din("xT", [D, NTOK])
    vecs_d = din("vecs", [128, L, NV])
    gfin_d = din("gfin", [128, DC])
    sta_d = din("sta", [L, 128, DC, 2, NS])
    stb_d = din("stb", [L, 128, DC, 3, NS])
    sth_d = din("sth", [L, 128, DC, NS])
    stf_d = din("stf", [L, 128, FC, 2, NS])
    rgw_d = din("rgw", [L, 128, 2, DC, 128])
    W1r = din("W1r", [L, DC, 128, DC * 5 * 128])
    W2r = din("W2r", [L, DC, 128, DC * 4 * 128])
    W3r = din("W3r", [L, 2, 128, DC * 512])
    W5r = din("W5r", [L, FC // 2, 128, DC * 2 * 256])
    W6r = din("W6r", [L, DC, 128, FC * 128])

    yT = dout("yT", [D, NTOK])
    o_a = dout("o_a", [L, 128, DC, 18])
    o_b = dout("o_b", [L, 128, DC, 19])
    o_h = dout("o_h", [L, 128, DC, 17])
    o_f = dout("o_f", [L, 128, FC, 18])
    sh_a = dout("sh_a", [L, 128, DC, NS])
    sh_b = dout("sh_b", [L, 128, DC, 2, NS])
    sh_f = dout("sh_f", [L, 128, FC, NS])

    dbg = dout("dbg", [8, 128, NT]) if debug else None

    with ExitStack() as st:
        def sbt(name, shape, dt):
            return st.enter_context(nc.sbuf_tensor(name, list(shape), dt))

        x_sb = sbt("x_sb", [128, DC, NCOL], F32)
        hn = sbt("hn", [128, DC, GT], BF16)
        region = sbt("region", [128, 24 * GT], BF16)
        ya = region[:, 0:8 * GT].rearrange("p (c n) -> p c n", c=DC)
        yb = region[:, 8 * GT:16 * GT].rearrange("p (c n) -> p c n", c=DC)
        mixed = region[:, 16 * GT:24 * GT].rearrange("p (c n) -> p c n", c=DC)
        zz = region[:, 0:FC * GT].rearrange("p (c n) -> p c n", c=FC)
        ystage = region[:, 0:2 * DC * NT].bitcast(F32).rearrange("p (c n) -> p c n", c=DC)
        sq = sbt("sq", [128, DC, NT], BF16)
        rstd = sbt("rstd", [128, NT], F32)
        slots = [sbt(f"slot{i}", [128, SLOT], BF16) for i in range(NSLOT)]
        rgw = sbt("rgw_sb", [128, 2, DC, 128], BF16)
        vecs = sbt("vecs_sb", [128, L, NV], F32)
        gfin = sbt("gfin_sb", [128, DC], F32)
        sc1 = sbt("sc1", [128, L, DC], F32)
        sc05 = sbt("sc05", [128, L, DC], F32)
        hba = sbt("hba", [128, L, DC], F32)
        hbx = sbt("hbx", [128, L, DC], F32)
        tmpv = [sbt(f"tmpv{i}", [128, L, DC], F32) for i in range(6)]
        sa_in = sbt("sa_in", [128, DC, 2, NS], F32)
        sb_in = sbt("sb_in", [128, DC, 3, NS], F32)
        sh_in = sbt("sh_in", [128, DC, NS], F32)
        sf_in = sbt("sf_in", [128, FC, 2, NS], F32)
        car_a = sbt("car_a", [128, DC, 2], F32)
        car_b = sbt("car_b", [128, DC, 3], F32)
        car_h = sbt("car_h", [128, DC], F32)
        car_f = sbt("car_f", [128, FC, 2], F32)
        ost_a = sbt("ost_a", [128, DC, 18], F32)
        ost_b = sbt("ost_b", [128, DC, 19], F32)
        ost_h = sbt("ost_h", [128, DC, 17], F32)
        ost_f = sbt("ost_f", [128, FC, 18], F32)
        ones_bf = sbt("ones_bf", [128, 128], BF16)
        epsv = sbt("epsv", [128, 1], F32)
        onev = sbt("onev", [128, 1], F32)
        NWB = 8
        WB = [sbt(f"wb{k}", [128, GT + 4], F32) for k in range(NWB)]
        xcb = sbt("xcb", [128, GT], BF16)
        WBG = sbt("wbg", [128, GT + 4], F32)
        Hb = [sbt(f"H{p}", [128, GT], F32) for p in range(2)]
        psa = st.enter_context(nc.psum_tensor("psa", [128, 8, 512], F32))

        def PS(b, n=NT):
            return psa[:, b, 0:n]

        def PSP(b):
            return psa[:, b:b + 2, 0:NT]

        S = Sched(nc, st)
        state = {"bank": 0, "slot": 0, "unit": 0}

        def next_bank():
            b = state["bank"]
            state["bank"] = (b + 1) % 8
            return b

        def next_pair():
            b = state["bank"]
            if b % 2:
                b = (b + 1) % 8
            state["bank"] = (b + 2) % 8
            return b

        def next_slot():
            s = state["slot"]
            state["slot"] = (s + 1) % NSLOT
            return s

        def ACT(fn, r, w):
            return S.op("act", fn, r, w)

        def DVE(fn, r, w):
            return S.op("dve", fn, r, w)

        out_toks = []

        S.dma("sp", lambda e: e.dma_start(out=vecs[:], in_=vecs_d), "vecs", writes=["vecs"])
        S.dma("sp", lambda e: e.dma_start(out=gfin[:], in_=gfin_d), "gfin", writes=["gfin"])
        xT_v = xT.rearrange("(c p) n -> p c n", p=128)
        for t in range(NTILES):
            hi = min((t + 1) * NT, NTOK)
            if t == 2:
                deferred_x = []
            fnx = (lambda e, t=t, hi=hi: e.dma_start(out=x_sb[:, :, t * NT:hi], in_=xT_v[:, :, t * NT:hi]))
            if t < 2:
                S.dma("sp", fnx, f"xin{t}", writes=[f"x.{c}.{t}" for c in range(DC)])
            else:
                deferred_x.append((fnx, t))
        for c in range(DC):
            DVE(lambda e, c=c: e.memset(x_sb[:, c, NTOK:NCOL], 0.0), [], [f"x.{c}.{NTILES - 1}"])
        DVE(lambda e: e.memset(ones_bf[:], 1.0 / D), [], ["ones"])
        DVE(lambda e: e.memset(epsv[:], EPS), [], ["epsv"])
        DVE(lambda e: e.memset(onev[:], 1.0), [], ["onev"])
        lam_v = vecs[:, :, V_LAM:V_LAM + DC]
        DVE(lambda e: e.tensor_scalar(tmpv[0][:], lam_v, -1.0, None, ALU.mult), ["vecs"], ["tv0"])
        ACT(lambda e: e.activation(tmpv[1][:], tmpv[0][:], AF.Abs), ["tv0"], ["tv1"])
        ACT(lambda e: e.activation(tmpv[1][:], tmpv[1][:], AF.Exp, scale=-1.0), ["tv1"], ["tv1"])
        tu, tu2, tp = tmpv[3], tmpv[4], tmpv[5]
        DVE(lambda e: e.tensor_scalar(tu[:], tmpv[1][:], 2.0, None, ALU.add), ["tv1"], ["tu"])
        DVE(lambda e: e.reciprocal(tu[:], tu[:]), ["tu"], ["tu"])
        DVE(lambda e: e.tensor_tensor(tu[:], tu[:], tmpv[1][:], ALU.mult), ["tu", "tv1"], ["tu"])
        DVE(lambda e: e.tensor_tensor(tu2[:], tu[:], tu[:], ALU.mult), ["tu"], ["tu2"])
        DVE(lambda e: e.tensor_scalar(tp[:], tu2[:], 1.0 / 11.0, 1.0 / 9.0, ALU.mult, ALU.add), ["tu2"], ["tp"])
        for coef in (1.0 / 7.0, 1.0 / 5.0, 1.0 / 3.0, 1.0):
            DVE(lambda e: e.tensor_tensor(tp[:], tp[:], tu2[:], ALU.mult), ["tp", "tu2"], ["tp"])
            DVE(lambda e, coef=coef: e.tensor_scalar(tp[:], tp[:], coef, None, ALU.add), ["tp"], ["tp"])
        DVE(lambda e: e.scalar_tensor_tensor(tmpv[1][:], tp[:], 2.0, tu[:], ALU.mult, ALU.mult), ["tp", "tu"], ["tv1"])
        DVE(lambda e: e.tensor_scalar(tmpv[2][:], tmpv[0][:], 0.0, None, ALU.max), ["tv0"], ["tv2"])
        DVE(lambda e: e.tensor_tensor(tmpv[2][:], tmpv[2][:], tmpv[1][:], ALU.add), ["tv1", "tv2"], ["tv2"])
        DVE(lambda e: e.tensor_scalar(sc1[:], tmpv[2][:], -8.0, None, ALU.mult), ["tv2"], ["sc1"])
        DVE(lambda e: e.tensor_scalar(sc05[:], tmpv[2][:], -4.0, None, ALU.mult), ["tv2"], ["sc05"])
        DVE(lambda e: e.tensor_scalar(hba[:], vecs[:, :, V_RBA:V_RBA + DC], 0.5, None, ALU.mult), ["vecs"], ["hba"])
        DVE(lambda e: e.tensor_scalar(hbx[:], vecs[:, :, V_RBX:V_RBX + DC], 0.5, None, ALU.mult), ["vecs"], ["hbx"])
        VEC = ["vecs", "sc1", "sc05", "hba", "hbx", "gfin"]
        YA_ALL = [f"rg.{k}.{q}" for k in range(DC) for q in range(GT // NT)]

        def tile_cols(t):
            return t * NT, (t + 1) * NT

        def npr(t):
            return NPL if t == NTILES - 1 else NT

        def POOL(fn, r, w):
            return S.op("pool", fn, r, w)

        def mm_group(bank, lhs_list, rhs_list, reads, n=NT, first=True, last=True):
            def fn(e):
                ins = None
                nk = len(lhs_list)
                for k in range(nk):
                    ins = e.matmul(PS(bank, n), lhs_list[k], rhs_list[k],
                                   start=(first and k == 0), stop=(last and k == nk - 1))
                return ins
            if not plan["planning"] and first:
                key = f"ps{bank}"
                assert not (key in S.lastw and not S.readers.get(key)), \
                    f"PSUM bank {bank} overwritten before its consumer was recorded"
            S.op("pe", fn, reads, [f"ps{bank}"])

        def norm_sq(t, c_lo, c_hi):
            c0, c1 = tile_cols(t)
            ACT(lambda e: e.activation(sq[:, c_lo:c_hi, :], x_sb[:, c_lo:c_hi, c0:c1], AF.Square),
                [f"x.{c}.{t}" for c in range(c_lo, c_hi)], [f"sq.{c}" for c in range(c_lo, c_hi)])

        def norm_stats():
            b = next_bank()
            mm_group(b, [ones_bf[:]] * DC, [sq[:, c, :] for c in range(DC)], ["ones"] + [f"sq.{c}" for c in range(DC)])
            ACT(lambda e: e.activation(rstd[:], PS(b), AF.Ln, bias=epsv[:, 0:1]), [f"ps{b}", "epsv"], ["rstd"])
            ACT(lambda e: e.activation(rstd[:], rstd[:], AF.Exp, scale=-0.5), ["rstd"], ["rstd"])

        def norm_scale(l, t, lt, gcol, dst_is_hn=True):
            c0, c1 = tile_cols(t)
            for c in range(DC):
                if dst_is_hn:
                    dst = hn[:, c, lt * NT:(lt + 1) * NT]
                    wr = [f"hn.{c}.{lt}"]
                    g = vecs[:, l, gcol + c:gcol + c + 1]
                else:
                    dst = ystage[:, c, :]
                    wr = [f"ys.{c}"] + YA_ALL
                    g = gfin[:, c:c + 1]
                DVE(lambda e: e.scalar_tensor_tensor(
                    dst, x_sb[:, c, c0:c1], g, rstd[:], ALU.mult, ALU.mult),
                    [f"x.{c}.{t}", "rstd"] + VEC, wr)

        def rmsnorm(l, t, lt, gcol, dst_is_hn=True):
            norm_sq(t, 0, DC)
            norm_stats()
            norm_scale(l, t, lt, gcol, dst_is_hn)

        LA = NSLOT - 1

        def issue_load(j):
            si = j % NSLOT
            for (vf, src) in plan["loads"][j]:
                S.dma("pool", lambda e: e.dma_start(out=vf(slots[si]), in_=src),
                      f"slot{si}", writes=[f"slot{si}"])

        def load_slot(dmas, hold=0):
            if plan["planning"]:
                plan["loads"].append(dmas)
                return (len(plan["loads"]) - 1) % NSLOT
            i = state["li"]
            state["li"] += 1
            while state["issued"] < min(len(plan["loads"]), i + LA + 1 - hold):
                issue_load(state["issued"])
                state["issued"] += 1
            return i % NSLOT

        def conv_taps(n, lastg, src, dst, taps, bias, sample_bufs, rd, wr_name, which="all", wr_extra=()):
            W = len(taps)
            segs = [(0, n, None)]
            if lastg:
                segs.append((n, n + NS, sample_bufs))
            for (a, b_, sb_) in segs:
                def inp(k):
                    if sb_ is None or k == W - 1:
                        return src[:, a + k:b_ + k]
                    return sb_[k]
                i0 = inp(0)
                if which in ("all", "first"):
                    if bias is not None:
                        ACT(lambda e: e.activation(dst[:, a:b_], i0, AF.Identity, bias=bias, scale=taps[0]),
                            rd + VEC, [wr_name] + list(wr_extra))
                    else:
                        ACT(lambda e: e.activation(dst[:, a:b_], i0, AF.Identity, scale=taps[0]),
                            rd + VEC, [wr_name] + list(wr_extra))
                if which in ("all", "rest"):
                    for k in range(1, W):
                        ik = inp(k)
                        DVE(lambda e: e.scalar_tensor_tensor(
                            dst[:, a:b_], ik, taps[k], dst[:, a:b_], ALU.mult, ALU.add),
                            rd + VEC + [wr_name], [wr_name] + list(wr_extra))

        def halo_in(gi, buf, hw, car, resn, carn):
            if gi == 0:
                POOL(lambda e: e.memset(buf[:, 0:hw], 0.0), [], [resn + "h"])
            else:
                POOL(lambda e: e.tensor_copy(buf[:, 0:hw], car), [carn], [resn + "h"])

        def gen_program():
            for l in range(L):
                if not plan["planning"]:
                    gen_layer_prologue(l)
                for gi, tiles in enumerate(GROUPS):
                    gen_group(l, gi, tiles)

        def gen_layer_prologue(l):
            S.dma("sp", lambda e: e.dma_start(out=sa_in[:], in_=sta_d[l]), "st_sa_in", writes=["sa_in"])
            S.dma("sp", lambda e: e.dma_start(out=sb_in[:], in_=stb_d[l]), "st_sb_in", writes=["sb_in"])
            S.dma("sp", lambda e: e.dma_start(out=sh_in[:], in_=sth_d[l]), "st_sh_in", writes=["sh_in"])
            S.dma("sp", lambda e: e.dma_start(out=sf_in[:], in_=stf_d[l]), "st_sf_in", writes=["sf_in"])
            S.dma("pool", lambda e: e.dma_start(out=rgw[:], in_=rgw_d[l]), "rgw", writes=["rgw"])
            out_toks.append(S.dma("sp", lambda e: e.dma_start(out=sh_a[l], in_=sa_in[:, :, 1, :]),
                                  "o_sha", reads=["sa_in"]))
            out_toks.append(S.dma("sp", lambda e: e.dma_start(out=sh_b[l], in_=sb_in[:, :, 1:3, :]),
                                  "o_shb", reads=["sb_in"]))
            out_toks.append(S.dma("sp", lambda e: e.dma_start(out=sh_f[l], in_=sf_in[:, :, 1, :]),
                                  "o_shf", reads=["sf_in"]))

        def gen_group(l, gi, tiles):
            ntl = len(tiles)
            assert ntl == 2
            lastg = (gi == len(GROUPS) - 1)
            n = (NT + NPL) if lastg else GT
            SEg = n + NS
            g0 = gi * GT
            hn_r = lambda lt: [f"hn.{k}.{lt}" for k in range(DC)]

            def v2(ap2d):
                return ap2d.rearrange("p (t n) -> p t n", t=2)

            def pair_mm(lhs_fn, src, names, extra, pair=None):
                b = next_pair() if pair is None else 2 * (pair % 4)
                nk = src.shape[1]
                for lt in range(2):
                    mm_group(b + lt, [lhs_fn(k) for k in range(nk)],
                             [src[:, k, lt * NT:(lt + 1) * NT] for k in range(nk)],
                             extra + [names(k, lt) for k in range(nk)])
                return b

            def PSr(b):
                return [f"ps{b}", f"ps{b + 1}"]

            if l == 0 and gi == 0:
                for lt, t in enumerate(tiles):
                    rmsnorm(l, t, lt, V_G1)
                if not plan["planning"]:
                    for (fnx, t) in deferred_x:
                        S.dma("sp", fnx, f"xin{t}", reads=[f"hn.0.1"], writes=[f"x.{c}.{t}" for c in range(DC)])

            v1 = lambda sl: sl[:, 0:DC * 5 * 128].rearrange("p (kc s j) -> p kc s j", kc=DC, s=5)
            gCs, ua, xr, xc, Ab, Mb, Bb = WB[0:7]
            Gbs = [WB[7], WBG]
            hnn = lambda k, lt: f"hn.{k}.{lt}"
            chunk = {}

            def pe1(c):
                si = load_slot([(lambda sl: sl[:, 0:DC * 5 * 128], W1r[l, c])])
                W1 = v1(slots[si])
                sl_ = [f"slot{si}"]
                base = pb0 + c
                bg = pair_mm(lambda k: W1[:, k, 4, :], hn, hnn, sl_, pair=base)
                bx = pair_mm(lambda k: W1[:, k, 3, :], hn, hnn, sl_, pair=base + 1)
                chunk[c] = {"W1": W1, "sl": sl_, "bg": bg, "bx": bx, "base": base}

            def pe2(c):
                W1, sl_ = chunk[c]["W1"], chunk[c]["sl"]
                base = chunk[c]["base"]
                chunk[c]["bc"] = pair_mm(lambda k: W1[:, k, 1, :], hn, hnn, sl_, pair=base + 2)
                chunk[c]["bh"] = pair_mm(lambda k: W1[:, k, 2, :], hn, hnn, sl_, pair=base + 3)
                chunk[c]["bb"] = pair_mm(lambda k: W1[:, k, 0, :], hn, hnn, sl_, pair=base)

            def front_a(c):
                ck = chunk[c]
                bg, bx = ck["bg"], ck["bx"]
                Gb = Gbs[c % 2]
                Gn = ("wb7" if c % 2 == 0 else "wbg")
                ACT(lambda e: e.activation(v2(Gb[:, 0:GT]), PSP(bg), AF.Gelu_apprx_tanh), PSr(bg), [Gn, Gn + "h"])
                halo_in(gi, xr, 3, car_b[:, c, :], "wb2", f"car_b.{c}")
                ACT(lambda e: e.copy(v2(xr[:, 3:3 + GT]), PSP(bx)), PSr(bx), ["wb2"])
                POOL(lambda e: e.tensor_copy(car_b[:, c, :], xr[:, n:n + 3]), ["wb2", "wb2h"], [f"car_b.{c}"])
                if lastg:
                    POOL(lambda e: e.tensor_copy(ost_b[:, c, :], xr[:, n:n + 3 + NS]), ["wb2", "wb2h"], ["ost_b"])

            def front_b(c, part):
                ck = chunk[c]
                bc, bh, bb = ck["bc"], ck["bh"], ck["bb"]
                ca = gCs
                tb = [vecs[:, l, V_CBW + c * 4 + k:V_CBW + c * 4 + k + 1] for k in range(4)]
                ta = [vecs[:, l, V_CAW + c * 3 + k:V_CAW + c * 3 + k + 1] for k in range(3)]
                sbb = [sb_in[:, c, 0, :], sb_in[:, c, 1, :], sb_in[:, c, 2, :]]
                sba = [sa_in[:, c, 0, :], sa_in[:, c, 1, :]]
                if part == 1:
                    ACT(lambda e: e.copy(v2(gCs[:, 0:GT]), PSP(bc)), PSr(bc), ["wb0", "wb0h"])
                    halo_in(gi, ua, 2, car_a[:, c, :], "wb1", f"car_a.{c}")
                    DVE(lambda e: e.tensor_tensor(v2(ua[:, 2:2 + GT]), v2(gCs[:, 0:GT]), PSP(bh), ALU.mult),
                        PSr(bh) + ["wb0", "wb0h"], ["wb1"])
                    POOL(lambda e: e.tensor_copy(car_a[:, c, :], ua[:, n:n + 2]), ["wb1", "wb1h"], [f"car_a.{c}"])
                    if lastg:
                        POOL(lambda e: e.tensor_copy(ost_a[:, c, :], ua[:, n:n + 2 + NS]), ["wb1", "wb1h"], ["ost_a"])
                    conv_taps(n, lastg, xr, xc, tb, vecs[:, l, V_CBB + c:V_CBB + c + 1], sbb,
                              ["wb2", "wb2h", "sb_in"], "wb3", which="first", wr_extra=["wb3h"])
                    conv_taps(n, lastg, ua, ca, ta, None, sba, ["wb1", "wb1h", "sa_in"], "wb0", which="first", wr_extra=["wb0h"])
                    conv_taps(n, lastg, xr, xc, tb, vecs[:, l, V_CBB + c:V_CBB + c + 1], sbb,
                              ["wb2", "wb2h", "sb_in"], "wb3", which="rest", wr_extra=["wb3h"])
                else:
                    ACT(lambda e: e.copy(xcb[:, 0:GT], xc[:, 0:GT]), ["wb3", "wb3h"], ["xcb"])
                    conv_taps(n, lastg, ua, ca, ta, None, sba, ["wb1", "wb1h", "sa_in"], "wb0", which="rest", wr_extra=["wb0h"])
                    DVE(lambda e: e.tensor_tensor(v2(ya[:, c, 0:GT]), PSP(bb), v2(ca[:, 0:GT]), ALU.mult),
                        PSr(bb) + ["wb0", "wb0h"], [f"rg.{c}.0", f"rg.{c}.1"])

            def gates_pe(c):
                base = chunk[c]["base"]
                br = 2 * ((base + 3) % 4)
                for lt in range(2):
                    mm_group(br + lt, [rgw[:, 0, c, :]], [xcb[:, lt * NT:(lt + 1) * NT]], ["rgw", "xcb"])
                bi = 2 * (base % 4)
                for lt in range(2):
                    mm_group(bi + lt, [rgw[:, 1, c, :]], [xcb[:, lt * NT:(lt + 1) * NT]], ["rgw", "xcb"])
                chunk[c]["br"], chunk[c]["bi"] = br, bi

            def tail_elem(c, part):
                br, bi = chunk[c]["br"], chunk[c]["bi"]
                Hc = Hb[c % 2]
                Hn = f"H{c % 2}"
                Gb = Gbs[c % 2]
                Gn = ("wb7" if c % 2 == 0 else "wbg")
                if part == 1:
                    ACT(lambda e: e.activation(v2(Ab[:, 0:GT]), PSP(br), AF.Tanh, bias=hba[:, l, c:c + 1], scale=0.5),
                        PSr(br) + VEC, ["wb4", "wb4h"])
                    ACT(lambda e: e.activation(v2(Bb[:, 0:GT]), PSP(bi), AF.Tanh, bias=hbx[:, l, c:c + 1], scale=0.5),
                        PSr(bi) + VEC, ["wb6", "wb6h"])
                    DVE(lambda e: e.scalar_tensor_tensor(Bb[:, 0:GT], Bb[:, 0:GT], 1.0, xc[:, 0:GT], ALU.add, ALU.mult),
                        ["wb6", "wb6h", "wb3", "wb3h"], ["wb6", "wb6h"])
                elif part == 2:
                    ACT(lambda e: e.activation(Ab[:, 0:GT], Ab[:, 0:GT], AF.Exp,
                                               bias=sc05[:, l, c:c + 1], scale=sc05[:, l, c:c + 1]),
                        ["wb4", "wb4h"] + VEC, ["wb4", "wb4h"])
                    DVE(lambda e: e.tensor_tensor(Mb[:, 0:GT], Ab[:, 0:GT], Ab[:, 0:GT], ALU.mult), ["wb4", "wb4h"], ["wb5", "wb5h"])
                    ACT(lambda e: e.activation(Mb[:, 0:GT], Mb[:, 0:GT], AF.Sqrt, bias=onev[:, 0:1], scale=-1.0),
                        ["wb5", "wb5h", "onev"], ["wb5", "wb5h"])
                    if gi == 0:
                        POOL(lambda e: e.memset(Mb[:, 0:1], 1.0), ["wb5", "wb5h"], ["wb5", "wb5h"])
                else:
                    DVE(lambda e: e.scalar_tensor_tensor(Bb[:, 0:GT], Bb[:, 0:GT], 0.5, Mb[:, 0:GT], ALU.mult, ALU.mult),
                        ["wb6", "wb6h", "wb5", "wb5h"], ["wb6", "wb6h"])
                    if gi == 0:
                        DVE(lambda e: e.tensor_tensor_scan(Hc[:, 0:n], Ab[:, 0:n], Bb[:, 0:n], 0.0, ALU.mult, ALU.add),
                            ["wb4", "wb4h", "wb6", "wb6h"], [Hn])
                    else:
                        DVE(lambda e: e.tensor_tensor_scan(Hc[:, 0:n], Ab[:, 0:n], Bb[:, 0:n], car_h[:, c:c + 1],
                                                           ALU.mult, ALU.add),
                            ["wb4", "wb4h", "wb6", "wb6h", f"car_h.{c}"], [Hn])
                    POOL(lambda e: e.tensor_copy(car_h[:, c:c + 1], Hc[:, n - 1:n]), [Hn], [f"car_h.{c}"])
                    if lastg:
                        hs_ = Hc[:, n:SEg]
                        POOL(lambda e: e.tensor_tensor(hs_, Ab[:, n:SEg], sh_in[:, c, :], ALU.mult),
                             ["wb4", "wb4h", "sh_in"], [Hn + "s"])
                        POOL(lambda e: e.tensor_tensor(hs_, hs_, Bb[:, n:SEg], ALU.add), ["wb6", "wb6h", Hn + "s"], [Hn + "s"])
                        POOL(lambda e: e.tensor_copy(ost_h[:, c, :], Hc[:, n - 1:SEg]), [Hn, Hn + "s"], ["ost_h"])
                    DVE(lambda e: e.tensor_tensor(yb[:, c, 0:GT], Gb[:, 0:GT], Hc[:, 0:GT], ALU.mult),
                        [Gn, Gn + "h", Hn, Hn + "s"], [f"rg.{8 + c}.0", f"rg.{8 + c}.1"])
                    if debug and (not plan["planning"]) and l == 0 and c == 0 and gi == 0:
                        for di, (bufn, resn) in enumerate(((Ab, "wb4"), (Mb, "wb5"), (Bb, "wb6"), (xc, "wb3"), (Hc, Hn), (xr, "wb2"))):
                            out_toks.append(S.dma("sp", lambda e: e.dma_start(out=dbg[di], in_=bufn[:, 0:NT]),
                                                  f"dbg{di}", reads=[resn]))

            pb0 = next_pair() // 2
            pe1(0)
            front_a(0)
            pe2(0)
            front_b(0, 1)
            front_b(0, 2)
            for c in range(DC):
                if c + 1 < DC:
                    pe1(c + 1)
                gates_pe(c)
                tail_elem(c, 1)
                tail_elem(c, 2)
                if c + 1 < DC:
                    front_a(c + 1)
                    pe2(c + 1)
                    front_b(c + 1, 1)
                tail_elem(c, 3)
                if c + 1 < DC:
                    front_b(c + 1, 2)

            if lastg:
                out_toks.append(S.dma("sp", lambda e: e.dma_start(out=o_a[l], in_=ost_a[:]), "o_a", reads=["ost_a"]))
                out_toks.append(S.dma("sp", lambda e: e.dma_start(out=o_b[l], in_=ost_b[:]), "o_b", reads=["ost_b"]))
                out_toks.append(S.dma("sp", lambda e: e.dma_start(out=o_h[l], in_=ost_h[:]), "o_h", reads=["ost_h"]))

            v4 = lambda sl: sl[:, 0:DC * 4 * 128].rearrange("p (kc s j) -> p kc s j", kc=DC, s=4)
            p2 = {}

            def p2_front(m, hold):
                si = load_slot([(lambda sl: sl[:, 0:DC * 4 * 128], W2r[l, m])], hold=hold)
                W2 = v4(slots[si])
                sl_ = [f"slot{si}"]
                q = m % 2
                sa_b, sb_b = WB[2 * q], WB[2 * q + 1]
                sa_n, sb_n = f"wb{2 * q}", f"wb{2 * q + 1}"
                b_ma = pair_mm(lambda k: W2[:, k, 2, :], hn, lambda k, lt: f"hn.{k}.{lt}", sl_)
                b_mb = pair_mm(lambda k: W2[:, k, 3, :], hn, lambda k, lt: f"hn.{k}.{lt}", sl_)
                ACT(lambda e: e.activation(v2(sa_b[:, 0:GT]), PSP(b_ma), AF.Sigmoid,
                                           bias=vecs[:, l, V_BG + m:V_BG + m + 1]),
                    PSr(b_ma) + VEC, [sa_n, sa_n + "h"])
                ACT(lambda e: e.activation(v2(sb_b[:, 0:GT]), PSP(b_mb), AF.Sigmoid,
                                           bias=vecs[:, l, V_BG + 8 + m:V_BG + 8 + m + 1]),
                    PSr(b_mb) + VEC, [sb_n, sb_n + "h"])
                p2[m] = (W2, sl_, sa_b, sb_b, sa_n, sb_n)

            def p2_back(m):
                W2, sl_, sa_b, sb_b, sa_n, sb_n = p2[m]
                b_ya = pair_mm(lambda k: W2[:, k, 0, :], ya, lambda k, lt: f"rg.{k}.{lt}", sl_)
                b_yb = pair_mm(lambda k: W2[:, k, 1, :], yb, lambda k, lt: f"rg.{8 + k}.{lt}", sl_)
                DVE(lambda e: e.tensor_tensor(v2(sa_b[:, 0:GT]), v2(sa_b[:, 0:GT]), PSP(b_ya), ALU.mult),
                    PSr(b_ya) + [sa_n], [sa_n, sa_n + "h"])
                DVE(lambda e: e.tensor_tensor(v2(sb_b[:, 0:GT]), v2(sb_b[:, 0:GT]), PSP(b_yb), ALU.mult),
                    PSr(b_yb) + [sb_n], [sb_n, sb_n + "h"])
                DVE(lambda e: e.tensor_tensor(mixed[:, m, 0:GT], sa_b[:, 0:GT], sb_b[:, 0:GT], ALU.add),
                    [sa_n, sb_n], [f"rg.{16 + m}.0", f"rg.{16 + m}.1"])

            p2_front(0, 0)
            for m in range(DC):
                if m + 1 < DC:
                    p2_front(m + 1, 1)
                p2_back(m)

            v3 = lambda sl: sl[:, 0:DC * 512].rearrange("p (kc j) -> p kc j", kc=DC)
            sis = [load_slot([(lambda sl: sl[:, 0:DC * 512], W3r[l, half])], hold=half) for half in range(2)]
            for lt, t in enumerate(tiles):
                cs = slice(lt * NT, (lt + 1) * NT)
                c0, c1 = tile_cols(t)
                for hf_ in range(2):
                    evacs = []
                    for o in range(hf_ * 4, hf_ * 4 + 4):
                        W3 = v3(slots[sis[o // 4]])
                        oo = o % 4
                        b = next_bank()
                        mm_group(b, [W3[:, k, oo * 128:(oo + 1) * 128] for k in range(DC)],
                                 [mixed[:, k, cs] for k in range(DC)],
                                 [f"slot{sis[o // 4]}"] + [f"rg.{16 + k}.{lt}" for k in range(DC)])
                        evacs.append((b, o))
                    if lt > 0 and hf_ == 0:
                        norm_stats()
                    for (b, o) in evacs:
                        DVE(lambda e: e.tensor_tensor(
                            x_sb[:, o, c0:c1], PS(b), x_sb[:, o, c0:c1], ALU.add),
                            [f"ps{b}", f"x.{o}.{t}"], [f"x.{o}.{t}"])
                    norm_sq(t, hf_ * 4, hf_ * 4 + 4)
                    if lt > 0 and hf_ == 1:
                        norm_scale(l, tiles[lt - 1], lt - 1, V_G2)
            norm_stats()
            norm_scale(l, tiles[-1], ntl - 1, V_G2)

            v5 = lambda sl: sl[:, 0:DC * 2 * 256].rearrange("p (kc s j) -> p kc s j", kc=DC, s=2)
            pend = []
            for f2 in range(FC // 2):
                si = load_slot([(lambda sl: sl[:, 0:DC * 2 * 256], W5r[l, f2])])
                W5 = v5(slots[si])
                sl_ = [f"slot{si}"]
                for fi in range(2):
                    f = 2 * f2 + fi
                    q = f % 2
                    ub, ucb = WB[2 * q], WB[2 * q + 1]
                    ubn, ucn = f"wb{2 * q}", f"wb{2 * q + 1}"
                    b_u = pair_mm(lambda k: W5[:, k, 0, fi * 128:(fi + 1) * 128], hn, lambda k, lt: f"hn.{k}.{lt}", sl_)
                    b_g = pair_mm(lambda k: W5[:, k, 1, fi * 128:(fi + 1) * 128], hn, lambda k, lt: f"hn.{k}.{lt}", sl_)
                    halo_in(gi, ub, 2, car_f[:, f, :], ubn, f"car_f.{f}")
                    ACT(lambda e: e.copy(v2(ub[:, 2:2 + GT]), PSP(b_u)), PSr(b_u), [ubn])
                    POOL(lambda e: e.tensor_copy(car_f[:, f, :], ub[:, n:n + 2]), [ubn, ubn + "h"], [f"car_f.{f}"])
                    if lastg:
                        POOL(lambda e: e.tensor_copy(ost_f[:, f, :], ub[:, n:n + 2 + NS]), [ubn, ubn + "h"], ["ost_f"])
                    conv_taps(n, lastg, ub, ucb, [vecs[:, l, V_FCW + f * 3 + k:V_FCW + f * 3 + k + 1] for k in range(3)],
                              vecs[:, l, V_FCB + f:V_FCB + f + 1],
                              [sf_in[:, f, 0, :], sf_in[:, f, 1, :]],
                              [ubn, ubn + "h", "sf_in"], ucn, wr_extra=[ucn + "h"])

                    def stage2(ucb=ucb, b_g=b_g, f=f, ucn=ucn):
                        ACT(lambda e: e.activation(ucb[:, 0:GT], ucb[:, 0:GT], AF.Silu), [ucn], [ucn, ucn + "h"])
                        DVE(lambda e: e.tensor_tensor(v2(zz[:, f, 0:GT]), v2(ucb[:, 0:GT]), PSP(b_g), ALU.mult),
                            PSr(b_g) + [ucn], [f"rg.{f}.0", f"rg.{f}.1"])
                    if pend:
                        pend.pop(0)()
                    pend.append(stage2)
            while pend:
                pend.pop(0)()
            if lastg:
                out_toks.append(S.dma("sp", lambda e: e.dma_start(out=o_f[l], in_=ost_f[:]), "o_f", reads=["ost_f"]))

            nxt = None
            if gi + 1 < len(GROUPS):
                nxt = (l, gi + 1)
            elif l + 1 < L:
                nxt = (l + 1, 0)
            if nxt is not None:
                for lt, t in enumerate(GROUPS[nxt[1]]):
                    rmsnorm(nxt[0], t, lt, V_G1)

            v6 = lambda sl: sl[:, 0:FC * 128].rearrange("p (kc j) -> p kc j", kc=FC)
            for o in range(DC):
                si = load_slot([(lambda sl: sl[:, 0:FC * 128], W6r[l, o])])
                W6 = v6(slots[si])
                if o == 0:
                    b = next_pair()
                    KS = FC - 2
                    for lt in range(2):
                        mm_group(b + lt, [W6[:, k, :] for k in range(KS)], [zz[:, k, lt * NT:(lt + 1) * NT] for k in range(KS)],
                                 [f"slot{si}"] + [f"rg.{k}.{lt}" for k in range(KS)], last=False)
                    for lt in range(2):
                        mm_group(b + lt, [W6[:, k, :] for k in range(KS, FC)],
                                 [zz[:, k, lt * NT:(lt + 1) * NT] for k in range(KS, FC)],
                                 [f"slot{si}"] + [f"rg.{k}.{lt}" for k in range(KS, FC)], first=False)
                else:
                    b = pair_mm(lambda k: W6[:, k, :], zz, lambda k, lt: f"rg.{k}.{lt}", [f"slot{si}"])
                xo = x_sb[:, o, g0:g0 + GT]
                DVE(lambda e: e.tensor_tensor(v2(xo), PSP(b), v2(xo), ALU.add),
                    PSr(b) + [f"x.{o}.{tiles[0]}", f"x.{o}.{tiles[1]}"], [f"x.{o}.{tiles[0]}", f"x.{o}.{tiles[1]}"])

            if l == L - 1:
                yT_v = yT.rearrange("(c p) n -> p c n", p=128)
                for lt, t in enumerate(tiles):
                    c0, c1 = tile_cols(t)
                    rmsnorm(l, t, lt, None, dst_is_hn=False)
                    c1r = min(c1, NTOK)
                    out_toks.append(S.dma("sp", lambda e: e.dma_start(out=yT_v[:, :, c0:c1r], in_=ystage[:, :, 0:c1r - c0]),
                                          "o_y", reads=[f"ys.{c}" for c in range(DC)] + YA_ALL))

        class _Null:
            def op(self, *a, **k):
                return ("x", 0)

            def dma(self, *a, **k):
                return ("x", 0)
        S_real = S
        plan = {"planning": True, "loads": []}
        state.update({"li": 0, "issued": 0})
        S = _Null()
        saved_bank = state["bank"]
        gen_program()
        plan["planning"] = False
        S = S_real
        state["bank"] = saved_bank
        out_toks.clear()
        gen_program()

        S.final_wait("sp", out_toks)
        S.emit()
    return nc


_NC_CACHE = {}


def _fm(v, nch):
    v = np.asarray(v)
    lead = v.shape[:-1]
    r = v.reshape(lead + (nch, 128))
    nd = r.ndim
    perm = (nd - 1, nd - 2) + tuple(range(nd - 2))
    return np.ascontiguousarray(r.transpose(perm))


def kernel(x_prompt, x_sample, state_conv_a, state_conv_b, state_rglru, state_conv_ffn,
           meta_tokens, norm_mix, norm_ffn, norm_final, w_in, b_gate, conv_a_w, w_a_out,
           conv_b_w, conv_b_b, rg_w_a, rg_b_a, rg_w_x, rg_b_x, rg_lambda, w_b_out, w_o,
           ffn_w_up, ffn_w_gate, ffn_conv_w, ffn_conv_b, ffn_w_down):
    f32 = np.float32
    x_prompt = np.asarray(x_prompt, f32)
    x_sample = np.asarray(x_sample, f32)
    ncores = 8
    vec = np.zeros((128, L, NV), f32)
    for l in range(L):
        vec[:, l, V_G1:V_G1 + 8] = _fm(np.asarray(norm_mix)[l], 8)
        vec[:, l, V_G2:V_G2 + 8] = _fm(np.asarray(norm_ffn)[l], 8)
        vec[:, l, V_BG:V_BG + 16] = _fm(np.asarray(b_gate)[l], 16)
        vec[:, l, V_CAW:V_CAW + 24] = _fm(np.asarray(conv_a_w)[l], 8).reshape(128, 24)
        vec[:, l, V_CBW:V_CBW + 32] = _fm(np.asarray(conv_b_w)[l], 8).reshape(128, 32)
        vec[:, l, V_CBB:V_CBB + 8] = _fm(np.asarray(conv_b_b)[l], 8)
        vec[:, l, V_RBA:V_RBA + 8] = _fm(np.asarray(rg_b_a)[l], 8)
        vec[:, l, V_RBX:V_RBX + 8] = _fm(np.asarray(rg_b_x)[l], 8)
        vec[:, l, V_LAM:V_LAM + 8] = _fm(np.asarray(rg_lambda)[l], 8)
        vec[:, l, V_FCW:V_FCW + 66] = _fm(np.asarray(ffn_conv_w)[l], 22).reshape(128, 66)
        vec[:, l, V_FCB:V_FCB + 22] = _fm(np.asarray(ffn_conv_b)[l], 22)
    gfin = _fm(np.asarray(norm_final), 8)
    rgw = np.zeros((L, 128, 2, 8, 128), f32)
    for wi, W in enumerate((np.asarray(rg_w_a), np.asarray(rg_w_x))):
        for c in range(8):
            for hh in range(2):
                rgw[:, hh * 64:(hh + 1) * 64, wi, c, hh * 64:(hh + 1) * 64] = W[:, 2 * c + hh]
    w_in_ = np.asarray(w_in, f32).reshape(L, DC, 128, 7, DC, 128)
    W1r = np.ascontiguousarray(w_in_[:, :, :, 0:5].transpose(0, 4, 2, 1, 3, 5)).reshape(L, DC, 128, DC * 5 * 128)
    wa_ = np.asarray(w_a_out, f32).reshape(L, DC, 128, DC, 128)
    wb_ = np.asarray(w_b_out, f32).reshape(L, DC, 128, DC, 128)
    W2r = np.empty((L, DC, 128, DC, 4, 128), f32)
    W2r[:, :, :, :, 0] = wa_.transpose(0, 3, 2, 1, 4)
    W2r[:, :, :, :, 1] = wb_.transpose(0, 3, 2, 1, 4)
    W2r[:, :, :, :, 2] = w_in_[:, :, :, 5].transpose(0, 3, 2, 1, 4)
    W2r[:, :, :, :, 3] = w_in_[:, :, :, 6].transpose(0, 3, 2, 1, 4)
    W2r = W2r.reshape(L, DC, 128, DC * 4 * 128)
    wo_ = np.asarray(w_o, f32).reshape(L, DC, 128, 2, 512)
    W3r = np.ascontiguousarray(wo_.transpose(0, 3, 2, 1, 4)).reshape(L, 2, 128, DC * 512)
    wu_ = np.asarray(ffn_w_up, f32).reshape(L, DC, 128, FC // 2, 256)
    wg_ = np.asarray(ffn_w_gate, f32).reshape(L, DC, 128, FC // 2, 256)
    W5r = np.empty((L, FC // 2, 128, DC, 2, 256), f32)
    W5r[:, :, :, :, 0] = wu_.transpose(0, 3, 2, 1, 4)
    W5r[:, :, :, :, 1] = wg_.transpose(0, 3, 2, 1, 4)
    W5r = W5r.reshape(L, FC // 2, 128, DC * 2 * 256)
    wd_ = np.asarray(ffn_w_down, f32).reshape(L, FC, 128, DC, 128)
    W6r = np.ascontiguousarray(wd_.transpose(0, 3, 2, 1, 4)).reshape(L, DC, 128, FC * 128)
    weights = {"W1r": W1r, "W2r": W2r, "W3r": W3r, "W5r": W5r, "W6r": W6r, "vecs": vec, "gfin": gfin, "rgw": rgw}
    sca = np.asarray(state_conv_a, f32)
    scb = np.asarray(state_conv_b, f32)
    srg = np.asarray(state_rglru, f32)
    scf = np.asarray(state_conv_ffn, f32)
    meta = np.asarray(meta_tokens, f32)
    in_maps = []
    for i in range(ncores):
        bs = slice(i * NS, (i + 1) * NS)
        xall = np.concatenate([meta, x_prompt[i], x_sample[bs, 0]], axis=0)
        m = dict(weights)
        m["xT"] = np.ascontiguousarray(xall.T)
        m["sta"] = np.ascontiguousarray(sca[:, bs].reshape(L, NS, 2, 8, 128).transpose(0, 4, 3, 2, 1))
        m["stb"] = np.ascontiguousarray(scb[:, bs].reshape(L, NS, 3, 8, 128).transpose(0, 4, 3, 2, 1))
        m["sth"] = np.ascontiguousarray(srg[:, bs].reshape(L, NS, 8, 128).transpose(0, 3, 2, 1))
        m["stf"] = np.ascontiguousarray(scf[:, bs].reshape(L, NS, 2, 22, 128).transpose(0, 4, 3, 2, 1))
        in_maps.append(m)

    if "nc" not in _NC_CACHE:
        _NC_CACHE["nc"] = build_nc()
    nc = _NC_CACHE["nc"]
    res = run_bass_kernel_spmd(nc, in_maps, core_ids=list(range(ncores)))
    R = res.results

    y_prompt = np.zeros((8, SEQ, D), f32)
    y_sample = np.zeros((128, 1, D), f32)
    p_a = np.zeros((L, 8, 2, D), f32)
    p_b = np.zeros((L, 8, 3, D), f32)
    p_h = np.zeros((L, 8, D), f32)
    p_f = np.zeros((L, 8, 2, DFF), f32)
    s_a = np.zeros((L, 128, 2, D), f32)
    s_b = np.zeros((L, 128, 3, D), f32)
    s_h = np.zeros((L, 128, D), f32)
    s_f = np.zeros((L, 128, 2, DFF), f32)

    def tm(a):
        Lh, P, nch, r = a.shape
        return a.transpose(0, 3, 2, 1).reshape(Lh, r, nch * P)

    for i in range(ncores):
        r = R[i]
        bs = slice(i * NS, (i + 1) * NS)
        yTi = np.asarray(r["yT"])
        y_prompt[i] = yTi[:, NMETA:NMETA + SEQ].T
        y_sample[bs, 0] = yTi[:, NMETA + SEQ:].T
        oa = tm(np.asarray(r["o_a"]))
        ob = tm(np.asarray(r["o_b"]))
        oh = tm(np.asarray(r["o_h"]))
        of = tm(np.asarray(r["o_f"]))
        p_a[:, i] = oa[:, 0:2]
        s_a[:, bs, 1] = oa[:, 2:18]
        p_b[:, i] = ob[:, 0:3]
        s_b[:, bs, 2] = ob[:, 3:19]
        p_h[:, i] = oh[:, 0]
        s_h[:, bs] = oh[:, 1:17]
        p_f[:, i] = of[:, 0:2]
        s_f[:, bs, 1] = of[:, 2:18]
        s_a[:, bs, 0] = tm(np.asarray(r["sh_a"]))
        shb = np.asarray(r["sh_b"])
        for k in range(2):
            s_b[:, bs, k] = tm(np.ascontiguousarray(shb[:, :, :, k, :]))
        s_f[:, bs, 0] = tm(np.asarray(r["sh_f"]))
    return (y_prompt, y_sample, p_a, p_b, p_h, p_f, s_a, s_b, s_h, s_f)
```

```python
from contextlib import ExitStack
import types
import numpy as np
import concourse.bass as bass
import concourse.mybir as mybir
from concourse.bass_utils import run_bass_kernel_spmd

F32 = mybir.dt.float32
BF16 = mybir.dt.bfloat16
AF = mybir.ActivationFunctionType
ALU = mybir.AluOpType

L = 4
D = 1024
DC = 8
DFF = 2816
FC = 22
DIN = 7168
NMETA = 16
SEQ = 2048
NS = 16
NTOK = NMETA + SEQ + NS
NT = 352
NTILES = 6
NPL = 304
SE = NPL + NS
NCOL = NTILES * NT
EPS = 1e-6
GROUPS = [[0, 1], [2, 3], [4, 5]]
GT = max(len(g) for g in GROUPS) * NT
NV = 208
SLOT = 5120
NSLOT = 3

ENGS = ("pe", "act", "dve", "pool", "sp")


def _freeze(fn):
    if fn is None or fn.__closure__ is None:
        return fn
    cells = []
    for c in fn.__closure__:
        try:
            cells.append(types.CellType(c.cell_contents))
        except ValueError:
            cells.append(c)
    return types.FunctionType(fn.__code__, fn.__globals__, fn.__name__, fn.__defaults__, tuple(cells))


class Sched:
    def __init__(self, nc, stack, same_engine_sync=True):
        self.nc = nc
        self.stack = stack
        self.same = same_engine_sync
        self.ops = {e: [] for e in ENGS}
        self.cnt = {}
        self.sems = {}
        self.seen = {e: {} for e in ENGS}
        self.lastw = {}
        self.readers = {}
        for e in ENGS:
            self._sem("E_" + e)

    def _sem(self, key):
        if key not in self.sems:
            self.sems[key] = self.stack.enter_context(self.nc.semaphore(key))
            self.cnt[key] = 0
        return self.sems[key]

    def _deps(self, reads, writes):
        deps = {}

        def add(tok):
            if tok is None:
                return
            k, v = tok
            if deps.get(k, 0) < v:
                deps[k] = v

        for r in reads:
            add(self.lastw.get(r))
        for w in writes:
            add(self.lastw.get(w))
            for t in self.readers.get(w, ()):
                add(t)
        return deps

    def _waits(self, eng, deps):
        waits = []
        seen = self.seen[eng]
        own = "E_" + eng
        for k, v in deps.items():
            if k == own and not self.same:
                continue
            if seen.get(k, 0) < v:
                seen[k] = v
                waits.append((k, v))
        return waits

    def _commit(self, tok, reads, writes):
        for r in reads:
            self.readers.setdefault(r, []).append(tok)
        for w in writes:
            self.lastw[w] = tok
            self.readers[w] = []

    def op(self, eng, fn, reads=(), writes=()):
        deps = self._deps(reads, writes)
        waits = self._waits(eng, deps)
        key = "E_" + eng
        self.cnt[key] += 1
        tok = (key, self.cnt[key])
        self.ops[eng].append((waits, _freeze(fn), key, 1))
        self._commit(tok, reads, writes)
        return tok

    def dma(self, eng, fn, semname, reads=(), writes=()):
        key = "D_" + semname
        self._sem(key)
        deps = self._deps(reads, writes)
        waits = self._waits(eng, deps)
        self.cnt[key] += 16
        tok = (key, self.cnt[key])
        self.ops[eng].append((waits, _freeze(fn), key, 16))
        self._commit(tok, reads, writes)
        return tok

    def final_wait(self, eng, toks):
        deps = {}
        for k, v in toks:
            if deps.get(k, 0) < v:
                deps[k] = v
        waits = self._waits(eng, deps)
        self.ops[eng].append((waits, None, None, 0))

    def emit(self):
        nc = self.nc
        S = self
        with nc.Block() as block:
            def run(e, h):
                for waits, fn, key, inc in S.ops[e]:
                    for k, v in waits:
                        h.wait_ge(S.sems[k], v)
                    if fn is not None:
                        ins = fn(h)
                        ins.then_inc(S.sems[key], inc)

            @block.tensor
            def _(h):
                run("pe", h)

            @block.scalar
            def _(h):
                run("act", h)

            @block.vector
            def _(h):
                run("dve", h)

            @block.gpsimd
            def _(h):
                run("pool", h)

            @block.sync
            def _(h):
                run("sp", h)


V_G1, V_G2, V_BG, V_CAW, V_CBW, V_CBB, V_RBA, V_RBX, V_LAM, V_FCW, V_FCB = (
    0, 8, 16, 32, 56, 88, 96, 104, 112, 120, 186)


def build_nc(debug=False):
    nc = bass.Bass("TRN2", target_bir_lowering=False)

    def din(name, shape):
        return nc.dram_tensor(name, list(shape), F32, kind="ExternalInput").ap()

    def dout(name, shape):
        return nc.dram_tensor(name, list(shape), F32, kind="ExternalOutput").ap()

    xT = din("xT", [D, NTOK])
    vecs_d = din("vecs", [128, L, NV])
    gfin_d = din("gfin", [128, DC])
    sta_d = din("sta", [L, 128, DC, 2, NS])
    stb_d = din("stb", [L, 128, DC, 3, NS])
    sth_d = din("sth", [L, 128, DC, NS])
    stf_d = din("stf", [L, 128, FC, 2, NS])
    rgw_d = din("rgw", [L, 128, 2, DC, 128])
    W1r = din("W1r", [L, DC, 128, DC * 5 * 128])
    W2r = din("W2r", [L, DC, 128, DC * 4 * 128])
    W3r = din("W3r", [L, 2, 128, DC * 512])
    W5r = din("W5r", [L, FC // 2, 128, DC * 2 * 256])
    W6r = din("W6r", [L, DC, 128, FC * 128])

    yT = dout("yT", [D, NTOK])
    o_a = dout("o_a", [L, 128, DC, 18])
    o_b = dout("o_b", [L, 128, DC, 19])
    o_h = dout("o_h", [L, 128, DC, 17])
    o_f = dout("o_f", [L, 128, FC, 18])
    sh_a = dout("sh_a", [L, 128, DC, NS])
    sh_b = dout("sh_b", [L, 128, DC, 2, NS])
    sh_f = dout("sh_f", [L, 128, FC, NS])

    dbg = dout("dbg", [8, 128, NT]) if debug else None

    with ExitStack() as st:
        def sbt(name, shape, dt):
            return st.enter_context(nc.sbuf_tensor(name, list(shape), dt))

        x_sb = sbt("x_sb", [128, DC, NCOL], F32)
        hn = sbt("hn", [128, DC, GT], BF16)
        region = sbt("region", [128, 24 * GT], BF16)
        ya = region[:, 0:8 * GT].rearrange("p (c n) -> p c n", c=DC)
        yb = region[:, 8 * GT:16 * GT].rearrange("p (c n) -> p c n", c=DC)
        mixed = region[:, 16 * GT:24 * GT].rearrange("p (c n) -> p c n", c=DC)
        zz = region[:, 0:FC * GT].rearrange("p (c n) -> p c n", c=FC)
        ystage = region[:, 0:2 * DC * NT].bitcast(F32).rearrange("p (c n) -> p c n", c=DC)
        sq = sbt("sq", [128, DC, NT], BF16)
        rstd = sbt("rstd", [128, NT], F32)
        slots = [sbt(f"slot{i}", [128, SLOT], BF16) for i in range(NSLOT)]
        rgw = sbt("rgw_sb", [128, 2, DC, 128], BF16)
        vecs = sbt("vecs_sb", [128, L, NV], F32)
        gfin = sbt("gfin_sb", [128, DC], F32)
        sc1 = sbt("sc1", [128, L, DC], F32)
        sc05 = sbt("sc05", [128, L, DC], F32)
        hba = sbt("hba", [128, L, DC], F32)
        hbx = sbt("hbx", [128, L, DC], F32)
        tmpv = [sbt(f"tmpv{i}", [128, L, DC], F32) for i in range(6)]
        sa_in = sbt("sa_in", [128, DC, 2, NS], F32)
        sb_in = sbt("sb_in", [128, DC, 3, NS], F32)
        sh_in = sbt("sh_in", [128, DC, NS], F32)
        sf_in = sbt("sf_in", [128, FC, 2, NS], F32)
        car_a = sbt("car_a", [128, DC, 2], F32)
        car_b = sbt("car_b", [128, DC, 3], F32)
        car_h = sbt("car_h", [128, DC], F32)
        car_f = sbt("car_f", [128, FC, 2], F32)
        ost_a = sbt("ost_a", [128, DC, 18], F32)
        ost_b = sbt("ost_b", [128, DC, 19], F32)
        ost_h = sbt("ost_h", [128, DC, 17], F32)
        ost_f = sbt("ost_f", [128, FC, 18], F32)
        ones_bf = sbt("ones_bf", [128, 128], BF16)
        epsv = sbt("epsv", [128, 1], F32)
        onev = sbt("onev", [128, 1], F32)
        NWB = 8
        WB = [sbt(f"wb{k}", [128, GT + 4], F32) for k in range(NWB)]
        xcb = sbt("xcb", [128, GT], BF16)
        WBG = sbt("wbg", [128, GT + 4], F32)
        Hb = [sbt(f"H{p}", [128, GT], F32) for p in range(2)]
        psa = st.enter_context(nc.psum_tensor("psa", [128, 8, 512], F32))

        def PS(b, n=NT):
            return psa[:, b, 0:n]

        def PSP(b):
            return psa[:, b:b + 2, 0:NT]

        S = Sched(nc, st)
        state = {"bank": 0, "slot": 0, "unit": 0}

        def next_bank():
            b = state["bank"]
            state["bank"] = (b + 1) % 8
            return b

        def next_pair():
            b = state["bank"]
            if b % 2:
                b = (b + 1) % 8
            state["bank"] = (b + 2) % 8
            return b

        def next_slot():
            s = state["slot"]
            state["slot"] = (s + 1) % NSLOT
            return s

        def ACT(fn, r, w):
            return S.op("act", fn, r, w)

        def DVE(fn, r, w):
            return S.op("dve", fn, r, w)

        out_toks = []

        S.dma("sp", lambda e: e.dma_start(out=vecs[:], in_=vecs_d), "vecs", writes=["vecs"])
        S.dma("sp", lambda e: e.dma_start(out=gfin[:], in_=gfin_d), "gfin", writes=["gfin"])
        xT_v = xT.rearrange("(c p) n -> p c n", p=128)
        for t in range(NTILES):
            hi = min((t + 1) * NT, NTOK)
            if t == 2:
                deferred_x = []
            fnx = (lambda e, t=t, hi=hi: e.dma_start(out=x_sb[:, :, t * NT:hi], in_=xT_v[:, :, t * NT:hi]))
            if t < 2:
                S.dma("sp", fnx, f"xin{t}", writes=[f"x.{c}.{t}" for c in range(DC)])
            else:
                deferred_x.append((fnx, t))
        for c in range(DC):
            DVE(lambda e, c=c: e.memset(x_sb[:, c, NTOK:NCOL], 0.0), [], [f"x.{c}.{NTILES - 1}"])
        DVE(lambda e: e.memset(ones_bf[:], 1.0 / D), [], ["ones"])
        DVE(lambda e: e.memset(epsv[:], EPS), [], ["epsv"])
        DVE(lambda e: e.memset(onev[:], 1.0), [], ["onev"])
        lam_v = vecs[:, :, V_LAM:V_LAM + DC]
        DVE(lambda e: e.tensor_scalar(tmpv[0][:], lam_v, -1.0, None, ALU.mult), ["vecs"], ["tv0"])
        ACT(lambda e: e.activation(tmpv[1][:], tmpv[0][:], AF.Abs), ["tv0"], ["tv1"])
        ACT(lambda e: e.activation(tmpv[1][:], tmpv[1][:], AF.Exp, scale=-1.0), ["tv1"], ["tv1"])
        tu, tu2, tp = tmpv[3], tmpv[4], tmpv[5]
        DVE(lambda e: e.tensor_scalar(tu[:], tmpv[1][:], 2.0, None, ALU.add), ["tv1"], ["tu"])
        DVE(lambda e: e.reciprocal(tu[:], tu[:]), ["tu"], ["tu"])
        DVE(lambda e: e.tensor_tensor(tu[:], tu[:], tmpv[1][:], ALU.mult), ["tu", "tv1"], ["tu"])
        DVE(lambda e: e.tensor_tensor(tu2[:], tu[:], tu[:], ALU.mult), ["tu"], ["tu2"])
        DVE(lambda e: e.tensor_scalar(tp[:], tu2[:], 1.0 / 11.0, 1.0 / 9.0, ALU.mult, ALU.add), ["tu2"], ["tp"])
        for coef in (1.0 / 7.0, 1.0 / 5.0, 1.0 / 3.0, 1.0):
            DVE(lambda e: e.tensor_tensor(tp[:], tp[:], tu2[:], ALU.mult), ["tp", "tu2"], ["tp"])
            DVE(lambda e, coef=coef: e.tensor_scalar(tp[:], tp[:], coef, None, ALU.add), ["tp"], ["tp"])
        DVE(lambda e: e.scalar_tensor_tensor(tmpv[1][:], tp[:], 2.0, tu[:], ALU.mult, ALU.mult), ["tp", "tu"], ["tv1"])
        DVE(lambda e: e.tensor_scalar(tmpv[2][:], tmpv[0][:], 0.0, None, ALU.max), ["tv0"], ["tv2"])
        DVE(lambda e: e.tensor_tensor(tmpv[2][:], tmpv[2][:], tmpv[1][:], ALU.add), ["tv1", "tv2"], ["tv2"])
        DVE(lambda e: e.tensor_scalar(sc1[:], tmpv[2][:], -8.0, None, ALU.mult), ["tv2"], ["sc1"])
        DVE(lambda e: e.tensor_scalar(sc05[:], tmpv[2][:], -4.0, None, ALU.mult), ["tv2"], ["sc05"])
        DVE(lambda e: e.tensor_scalar(hba[:], vecs[:, :, V_RBA:V_RBA + DC], 0.5, None, ALU.mult), ["vecs"], ["hba"])
        DVE(lambda e: e.tensor_scalar(hbx[:], vecs[:, :, V_RBX:V_RBX + DC], 0.5, None, ALU.mult), ["vecs"], ["hbx"])
        VEC = ["vecs", "sc1", "sc05", "hba", "hbx", "gfin"]
        YA_ALL = [f"rg.{k}.{q}" for k in range(DC) for q in range(GT // NT)]

        def tile_cols(t):
            return t * NT, (t + 1) * NT

        def npr(t):
            return NPL if t == NTILES - 1 else NT

        def POOL(fn, r, w):
            return S.op("pool", fn, r, w)

        def mm_group(bank, lhs_list, rhs_list, reads, n=NT, first=True, last=True):
            def fn(e):
                ins = None
                nk = len(lhs_list)
                for k in range(nk):
                    ins = e.matmul(PS(bank, n), lhs_list[k], rhs_list[k],
                                   start=(first and k == 0), stop=(last and k == nk - 1))
                return ins
            if not plan["planning"] and first:
                key = f"ps{bank}"
                assert not (key in S.lastw and not S.readers.get(key)), \
                    f"PSUM bank {bank} overwritten before its consumer was recorded"
            S.op("pe", fn, reads, [f"ps{bank}"])

        def norm_sq(t, c_lo, c_hi):
            c0, c1 = tile_cols(t)
            ACT(lambda e: e.activation(sq[:, c_lo:c_hi, :], x_sb[:, c_lo:c_hi, c0:c1], AF.Square),
                [f"x.{c}.{t}" for c in range(c_lo, c_hi)], [f"sq.{c}" for c in range(c_lo, c_hi)])

        def norm_stats():
            b = next_bank()
            mm_group(b, [ones_bf[:]] * DC, [sq[:, c, :] for c in range(DC)], ["ones"] + [f"sq.{c}" for c in range(DC)])
            ACT(lambda e: e.activation(rstd[:], PS(b), AF.Ln, bias=epsv[:, 0:1]), [f"ps{b}", "epsv"], ["rstd"])
            ACT(lambda e: e.activation(rstd[:], rstd[:], AF.Exp, scale=-0.5), ["rstd"], ["rstd"])

        def norm_scale(l, t, lt, gcol, dst_is_hn=True):
            c0, c1 = tile_cols(t)
            for c in range(DC):
                if dst_is_hn:
                    dst = hn[:, c, lt * NT:(lt + 1) * NT]
                    wr = [f"hn.{c}.{lt}"]
                    g = vecs[:, l, gcol + c:gcol + c + 1]
                else:
                    dst = ystage[:, c, :]
                    wr = [f"ys.{c}"] + YA_ALL
                    g = gfin[:, c:c + 1]
                DVE(lambda e: e.scalar_tensor_tensor(
                    dst, x_sb[:, c, c0:c1], g, rstd[:], ALU.mult, ALU.mult),
                    [f"x.{c}.{t}", "rstd"] + VEC, wr)

        def rmsnorm(l, t, lt, gcol, dst_is_hn=True):
            norm_sq(t, 0, DC)
            norm_stats()
            norm_scale(l, t, lt, gcol, dst_is_hn)

        LA = NSLOT - 1

        def issue_load(j):
            si = j % NSLOT
            for (vf, src) in plan["loads"][j]:
                S.dma("pool", lambda e: e.dma_start(out=vf(slots[si]), in_=src),
                      f"slot{si}", writes=[f"slot{si}"])

        def load_slot(dmas, hold=0):
            if plan["planning"]:
                plan["loads"].append(dmas)
                return (len(plan["loads"]) - 1) % NSLOT
            i = state["li"]
            state["li"] += 1
            while state["issued"] < min(len(plan["loads"]), i + LA + 1 - hold):
                issue_load(state["issued"])
                state["issued"] += 1
            return i % NSLOT

        def conv_taps(n, lastg, src, dst, taps, bias, sample_bufs, rd, wr_name, which="all", wr_extra=()):
            W = len(taps)
            segs = [(0, n, None)]
            if lastg:
                segs.append((n, n + NS, sample_bufs))
            for (a, b_, sb_) in segs:
                def inp(k):
                    if sb_ is None or k == W - 1:
                        return src[:, a + k:b_ + k]
                    return sb_[k]
                i0 = inp(0)
                if which in ("all", "first"):
                    if bias is not None:
                        ACT(lambda e: e.activation(dst[:, a:b_], i0, AF.Identity, bias=bias, scale=taps[0]),
                            rd + VEC, [wr_name] + list(wr_extra))
                    else:
                        ACT(lambda e: e.activation(dst[:, a:b_], i0, AF.Identity, scale=taps[0]),
                            rd + VEC, [wr_name] + list(wr_extra))
                if which in ("all", "rest"):
                    for k in range(1, W):
                        ik = inp(k)
                        DVE(lambda e: e.scalar_tensor_tensor(
                            dst[:, a:b_], ik, taps[k], dst[:, a:b_], ALU.mult, ALU.add),
                            rd + VEC + [wr_name], [wr_name] + list(wr_extra))

        def halo_in(gi, buf, hw, car, resn, carn):
            if gi == 0:
                POOL(lambda e: e.memset(buf[:, 0:hw], 0.0), [], [resn + "h"])
            else:
                POOL(lambda e: e.tensor_copy(buf[:, 0:hw], car), [carn], [resn + "h"])

        def gen_program():
            for l in range(L):
                if not plan["planning"]:
                    gen_layer_prologue(l)
                for gi, tiles in enumerate(GROUPS):
                    gen_group(l, gi, tiles)

        def gen_layer_prologue(l):
            S.dma("sp", lambda e: e.dma_start(out=sa_in[:], in_=sta_d[l]), "st_sa_in", writes=["sa_in"])
            S.dma("sp", lambda e: e.dma_start(out=sb_in[:], in_=stb_d[l]), "st_sb_in", writes=["sb_in"])
            S.dma("sp", lambda e: e.dma_start(out=sh_in[:], in_=sth_d[l]), "st_sh_in", writes=["sh_in"])
            S.dma("sp", lambda e: e.dma_start(out=sf_in[:], in_=stf_d[l]), "st_sf_in", writes=["sf_in"])
            S.dma("pool", lambda e: e.dma_start(out=rgw[:], in_=rgw_d[l]), "rgw", writes=["rgw"])
            out_toks.append(S.dma("sp", lambda e: e.dma_start(out=sh_a[l], in_=sa_in[:, :, 1, :]),
                                  "o_sha", reads=["sa_in"]))
            out_toks.append(S.dma("sp", lambda e: e.dma_start(out=sh_b[l], in_=sb_in[:, :, 1:3, :]),
                                  "o_shb", reads=["sb_in"]))
            out_toks.append(S.dma("sp", lambda e: e.dma_start(out=sh_f[l], in_=sf_in[:, :, 1, :]),
                                  "o_shf", reads=["sf_in"]))

        def gen_group(l, gi, tiles):
            ntl = len(tiles)
            assert ntl == 2
            lastg = (gi == len(GROUPS) - 1)
            n = (NT + NPL) if lastg else GT
            SEg = n + NS
            g0 = gi * GT
            hn_r = lambda lt: [f"hn.{k}.{lt}" for k in range(DC)]

            def v2(ap2d):
                return ap2d.rearrange("p (t n) -> p t n", t=2)

            def pair_mm(lhs_fn, src, names, extra, pair=None):
                b = next_pair() if pair is None else 2 * (pair % 4)
                nk = src.shape[1]
                for lt in range(2):
                    mm_group(b + lt, [lhs_fn(k) for k in range(nk)],
                             [src[:, k, lt * NT:(lt + 1) * NT] for k in range(nk)],
                             extra + [names(k, lt) for k in range(nk)])
                return b

            def PSr(b):
                return [f"ps{b}", f"ps{b + 1}"]

            if l == 0 and gi == 0:
                for lt, t in enumerate(tiles):
                    rmsnorm(l, t, lt, V_G1)
                if not plan["planning"]:
                    for (fnx, t) in deferred_x:
                        S.dma("sp", fnx, f"xin{t}", reads=[f"hn.0.1"], writes=[f"x.{c}.{t}" for c in range(DC)])

            v1 = lambda sl: sl[:, 0:DC * 5 * 128].rearrange("p (kc s j) -> p kc s j", kc=DC, s=5)
            gCs, ua, xr, xc, Ab, Mb, Bb = WB[0:7]
            Gbs = [WB[7], WBG]
            hnn = lambda k, lt: f"hn.{k}.{lt}"
            chunk = {}

            def pe1(c):
                si = load_slot([(lambda sl: sl[:, 0:DC * 5 * 128], W1r[l, c])])
                W1 = v1(slots[si])
                sl_ = [f"slot{si}"]
                base = pb0 + c
                bg = pair_mm(lambda k: W1[:, k, 4, :], hn, hnn, sl_, pair=base)
                bx = pair_mm(lambda k: W1[:, k, 3, :], hn, hnn, sl_, pair=base + 1)
                chunk[c] = {"W1": W1, "sl": sl_, "bg": bg, "bx": bx, "base": base}

            def pe2(c):
                W1, sl_ = chunk[c]["W1"], chunk[c]["sl"]
                base = chunk[c]["base"]
                chunk[c]["bc"] = pair_mm(lambda k: W1[:, k, 1, :], hn, hnn, sl_, pair=base + 2)
                chunk[c]["bh"] = pair_mm(lambda k: W1[:, k, 2, :], hn, hnn, sl_, pair=base + 3)
                chunk[c]["bb"] = pair_mm(lambda k: W1[:, k, 0, :], hn, hnn, sl_, pair=base)

            def front_a(c):
                ck = chunk[c]
                bg, bx = ck["bg"], ck["bx"]
                Gb = Gbs[c % 2]
                Gn = ("wb7" if c % 2 == 0 else "wbg")
                ACT(lambda e: e.activation(v2(Gb[:, 0:GT]), PSP(bg), AF.Gelu_apprx_tanh), PSr(bg), [Gn, Gn + "h"])
                halo_in(gi, xr, 3, car_b[:, c, :], "wb2", f"car_b.{c}")
                ACT(lambda e: e.copy(v2(xr[:, 3:3 + GT]), PSP(bx)), PSr(bx), ["wb2"])
                POOL(lambda e: e.tensor_copy(car_b[:, c, :], xr[:, n:n + 3]), ["wb2", "wb2h"], [f"car_b.{c}"])
                if lastg:
                    POOL(lambda e: e.tensor_copy(ost_b[:, c, :], xr[:, n:n + 3 + NS]), ["wb2", "wb2h"], ["ost_b"])

            def front_b(c):
                ck = chunk[c]
                bc, bh, bb = ck["bc"], ck["bh"], ck["bb"]
                ACT(lambda e: e.copy(v2(gCs[:, 0:GT]), PSP(bc)), PSr(bc), ["wb0", "wb0h"])
                halo_in(gi, ua, 2, car_a[:, c, :], "wb1", f"car_a.{c}")
                DVE(lambda e: e.tensor_tensor(v2(ua[:, 2:2 + GT]), v2(gCs[:, 0:GT]), PSP(bh), ALU.mult),
                    PSr(bh) + ["wb0", "wb0h"], ["wb1"])
                POOL(lambda e: e.tensor_copy(car_a[:, c, :], ua[:, n:n + 2]), ["wb1", "wb1h"], [f"car_a.{c}"])
                if lastg:
                    POOL(lambda e: e.tensor_copy(ost_a[:, c, :], ua[:, n:n + 2 + NS]), ["wb1", "wb1h"], ["ost_a"])
                ca = gCs
                tb = [vecs[:, l, V_CBW + c * 4 + k:V_CBW + c * 4 + k + 1] for k in range(4)]
                ta = [vecs[:, l, V_CAW + c * 3 + k:V_CAW + c * 3 + k + 1] for k in range(3)]
                sbb = [sb_in[:, c, 0, :], sb_in[:, c, 1, :], sb_in[:, c, 2, :]]
                sba = [sa_in[:, c, 0, :], sa_in[:, c, 1, :]]
                conv_taps(n, lastg, xr, xc, tb, vecs[:, l, V_CBB + c:V_CBB + c + 1], sbb,
                          ["wb2", "wb2h", "sb_in"], "wb3", which="first", wr_extra=["wb3h"])
                conv_taps(n, lastg, ua, ca, ta, None, sba, ["wb1", "wb1h", "sa_in"], "wb0", which="first", wr_extra=["wb0h"])
                conv_taps(n, lastg, xr, xc, tb, vecs[:, l, V_CBB + c:V_CBB + c + 1], sbb,
                          ["wb2", "wb2h", "sb_in"], "wb3", which="rest", wr_extra=["wb3h"])
                ACT(lambda e: e.copy(xcb[:, 0:GT], xc[:, 0:GT]), ["wb3", "wb3h"], ["xcb"])
                conv_taps(n, lastg, ua, ca, ta, None, sba, ["wb1", "wb1h", "sa_in"], "wb0", which="rest", wr_extra=["wb0h"])
                DVE(lambda e: e.tensor_tensor(v2(ya[:, c, 0:GT]), PSP(bb), v2(ca[:, 0:GT]), ALU.mult),
                    PSr(bb) + ["wb0", "wb0h"], [f"rg.{c}.0", f"rg.{c}.1"])

            def gates_pe(c):
                base = chunk[c]["base"]
                br = 2 * ((base + 3) % 4)
                for lt in range(2):
                    mm_group(br + lt, [rgw[:, 0, c, :]], [xcb[:, lt * NT:(lt + 1) * NT]], ["rgw", "xcb"])
                bi = 2 * (base % 4)
                for lt in range(2):
                    mm_group(bi + lt, [rgw[:, 1, c, :]], [xcb[:, lt * NT:(lt + 1) * NT]], ["rgw", "xcb"])
                chunk[c]["br"], chunk[c]["bi"] = br, bi

            def tail_elem(c):
                br, bi = chunk[c]["br"], chunk[c]["bi"]
                Hc = Hb[c % 2]
                Hn = f"H{c % 2}"
                Gb = Gbs[c % 2]
                Gn = ("wb7" if c % 2 == 0 else "wbg")
                ACT(lambda e: e.activation(v2(Ab[:, 0:GT]), PSP(br), AF.Tanh, bias=hba[:, l, c:c + 1], scale=0.5),
                    PSr(br) + VEC, ["wb4", "wb4h"])
                ACT(lambda e: e.activation(Ab[:, 0:GT], Ab[:, 0:GT], AF.Exp,
                                           bias=sc05[:, l, c:c + 1], scale=sc05[:, l, c:c + 1]),
                    ["wb4", "wb4h"] + VEC, ["wb4", "wb4h"])
                ACT(lambda e: e.activation(v2(Bb[:, 0:GT]), PSP(bi), AF.Tanh, bias=hbx[:, l, c:c + 1], scale=0.5),
                    PSr(bi) + VEC, ["wb6", "wb6h"])
                DVE(lambda e: e.tensor_tensor(Mb[:, 0:GT], Ab[:, 0:GT], Ab[:, 0:GT], ALU.mult), ["wb4", "wb4h"], ["wb5", "wb5h"])
                DVE(lambda e: e.scalar_tensor_tensor(Bb[:, 0:GT], Bb[:, 0:GT], 1.0, xc[:, 0:GT], ALU.add, ALU.mult),
                    ["wb6", "wb6h", "wb3", "wb3h"], ["wb6", "wb6h"])
                ACT(lambda e: e.activation(Mb[:, 0:GT], Mb[:, 0:GT], AF.Sqrt, bias=onev[:, 0:1], scale=-1.0),
                    ["wb5", "wb5h", "onev"], ["wb5", "wb5h"])
                if gi == 0:
                    POOL(lambda e: e.memset(Mb[:, 0:1], 1.0), ["wb5", "wb5h"], ["wb5", "wb5h"])
                DVE(lambda e: e.scalar_tensor_tensor(Bb[:, 0:GT], Bb[:, 0:GT], 0.5, Mb[:, 0:GT], ALU.mult, ALU.mult),
                    ["wb6", "wb6h", "wb5", "wb5h"], ["wb6", "wb6h"])
                if gi == 0:
                    DVE(lambda e: e.tensor_tensor_scan(Hc[:, 0:n], Ab[:, 0:n], Bb[:, 0:n], 0.0, ALU.mult, ALU.add),
                        ["wb4", "wb4h", "wb6", "wb6h"], [Hn])
                else:
                    DVE(lambda e: e.tensor_tensor_scan(Hc[:, 0:n], Ab[:, 0:n], Bb[:, 0:n], car_h[:, c:c + 1],
                                                       ALU.mult, ALU.add),
                        ["wb4", "wb4h", "wb6", "wb6h", f"car_h.{c}"], [Hn])
                POOL(lambda e: e.tensor_copy(car_h[:, c:c + 1], Hc[:, n - 1:n]), [Hn], [f"car_h.{c}"])
                if lastg:
                    hs_ = Hc[:, n:SEg]
                    POOL(lambda e: e.tensor_tensor(hs_, Ab[:, n:SEg], sh_in[:, c, :], ALU.mult),
                         ["wb4", "wb4h", "sh_in"], [Hn + "s"])
                    POOL(lambda e: e.tensor_tensor(hs_, hs_, Bb[:, n:SEg], ALU.add), ["wb6", "wb6h", Hn + "s"], [Hn + "s"])
                    POOL(lambda e: e.tensor_copy(ost_h[:, c, :], Hc[:, n - 1:SEg]), [Hn, Hn + "s"], ["ost_h"])
                DVE(lambda e: e.tensor_tensor(yb[:, c, 0:GT], Gb[:, 0:GT], Hc[:, 0:GT], ALU.mult),
                    [Gn, Gn + "h", Hn, Hn + "s"], [f"rg.{8 + c}.0", f"rg.{8 + c}.1"])
                if debug and (not plan["planning"]) and l == 0 and c == 0 and gi == 0:
                    for di, (bufn, resn) in enumerate(((Ab, "wb4"), (Mb, "wb5"), (Bb, "wb6"), (xc, "wb3"), (Hc, Hn), (xr, "wb2"))):
                        out_toks.append(S.dma("sp", lambda e: e.dma_start(out=dbg[di], in_=bufn[:, 0:NT]),
                                              f"dbg{di}", reads=[resn]))

            pb0 = next_pair() // 2
            pe1(0)
            front_a(0)
            pe2(0)
            front_b(0)
            for c in range(DC):
                if c + 1 < DC:
                    pe1(c + 1)
                gates_pe(c)
                tail_elem(c)
                if c + 1 < DC:
                    front_a(c + 1)
                    pe2(c + 1)
                    front_b(c + 1)

            if lastg:
                out_toks.append(S.dma("sp", lambda e: e.dma_start(out=o_a[l], in_=ost_a[:]), "o_a", reads=["ost_a"]))
                out_toks.append(S.dma("sp", lambda e: e.dma_start(out=o_b[l], in_=ost_b[:]), "o_b", reads=["ost_b"]))
                out_toks.append(S.dma("sp", lambda e: e.dma_start(out=o_h[l], in_=ost_h[:]), "o_h", reads=["ost_h"]))

            v4 = lambda sl: sl[:, 0:DC * 4 * 128].rearrange("p (kc s j) -> p kc s j", kc=DC, s=4)
            p2 = {}

            def p2_front(m, hold):
                si = load_slot([(lambda sl: sl[:, 0:DC * 4 * 128], W2r[l, m])], hold=hold)
                W2 = v4(slots[si])
                sl_ = [f"slot{si}"]
                q = m % 2
                sa_b, sb_b = WB[2 * q], WB[2 * q + 1]
                sa_n, sb_n = f"wb{2 * q}", f"wb{2 * q + 1}"
                b_ma = pair_mm(lambda k: W2[:, k, 2, :], hn, lambda k, lt: f"hn.{k}.{lt}", sl_)
                b_mb = pair_mm(lambda k: W2[:, k, 3, :], hn, lambda k, lt: f"hn.{k}.{lt}", sl_)
                ACT(lambda e: e.activation(v2(sa_b[:, 0:GT]), PSP(b_ma), AF.Sigmoid,
                                           bias=vecs[:, l, V_BG + m:V_BG + m + 1]),
                    PSr(b_ma) + VEC, [sa_n, sa_n + "h"])
                ACT(lambda e: e.activation(v2(sb_b[:, 0:GT]), PSP(b_mb), AF.Sigmoid,
                                           bias=vecs[:, l, V_BG + 8 + m:V_BG + 8 + m + 1]),
                    PSr(b_mb) + VEC, [sb_n, sb_n + "h"])
                p2[m] = (W2, sl_, sa_b, sb_b, sa_n, sb_n)

            def p2_back(m):
                W2, sl_, sa_b, sb_b, sa_n, sb_n = p2[m]
                b_ya = pair_mm(lambda k: W2[:, k, 0, :], ya, lambda k, lt: f"rg.{k}.{lt}", sl_)
                b_yb = pair_mm(lambda k: W2[:, k, 1, :], yb, lambda k, lt: f"rg.{8 + k}.{lt}", sl_)
                DVE(lambda e: e.tensor_tensor(v2(sa_b[:, 0:GT]), v2(sa_b[:, 0:GT]), PSP(b_ya), ALU.mult),
                    PSr(b_ya) + [sa_n], [sa_n, sa_n + "h"])
                DVE(lambda e: e.tensor_tensor(v2(sb_b[:, 0:GT]), v2(sb_b[:, 0:GT]), PSP(b_yb), ALU.mult),
                    PSr(b_yb) + [sb_n], [sb_n, sb_n + "h"])
                DVE(lambda e: e.tensor_tensor(mixed[:, m, 0:GT], sa_b[:, 0:GT], sb_b[:, 0:GT], ALU.add),
                    [sa_n, sb_n], [f"rg.{16 + m}.0", f"rg.{16 + m}.1"])

            p2_front(0, 0)
            for m in range(DC):
                if m + 1 < DC:
                    p2_front(m + 1, 1)
                p2_back(m)

            v3 = lambda sl: sl[:, 0:DC * 512].rearrange("p (kc j) -> p kc j", kc=DC)
            sis = [load_slot([(lambda sl: sl[:, 0:DC * 512], W3r[l, half])], hold=half) for half in range(2)]
            for lt, t in enumerate(tiles):
                cs = slice(lt * NT, (lt + 1) * NT)
                c0, c1 = tile_cols(t)
                for hf_ in range(2):
                    evacs = []
                    for o in range(hf_ * 4, hf_ * 4 + 4):
                        W3 = v3(slots[sis[o // 4]])
                        oo = o % 4
                        b = next_bank()
                        mm_group(b, [W3[:, k, oo * 128:(oo + 1) * 128] for k in range(DC)],
                                 [mixed[:, k, cs] for k in range(DC)],
                                 [f"slot{sis[o // 4]}"] + [f"rg.{16 + k}.{lt}" for k in range(DC)])
                        evacs.append((b, o))
                    if lt > 0 and hf_ == 0:
                        norm_stats()
                    for (b, o) in evacs:
                        DVE(lambda e: e.tensor_tensor(
                            x_sb[:, o, c0:c1], PS(b), x_sb[:, o, c0:c1], ALU.add),
                            [f"ps{b}", f"x.{o}.{t}"], [f"x.{o}.{t}"])
                    norm_sq(t, hf_ * 4, hf_ * 4 + 4)
                    if lt > 0 and hf_ == 1:
                        norm_scale(l, tiles[lt - 1], lt - 1, V_G2)
            norm_stats()
            norm_scale(l, tiles[-1], ntl - 1, V_G2)

            v5 = lambda sl: sl[:, 0:DC * 2 * 256].rearrange("p (kc s j) -> p kc s j", kc=DC, s=2)
            pend = []
            for f2 in range(FC // 2):
                si = load_slot([(lambda sl: sl[:, 0:DC * 2 * 256], W5r[l, f2])])
                W5 = v5(slots[si])
                sl_ = [f"slot{si}"]
                for fi in range(2):
                    f = 2 * f2 + fi
                    q = f % 2
                    ub, ucb = WB[2 * q], WB[2 * q + 1]
                    ubn, ucn = f"wb{2 * q}", f"wb{2 * q + 1}"
                    b_u = pair_mm(lambda k: W5[:, k, 0, fi * 128:(fi + 1) * 128], hn, lambda k, lt: f"hn.{k}.{lt}", sl_)
                    b_g = pair_mm(lambda k: W5[:, k, 1, fi * 128:(fi + 1) * 128], hn, lambda k, lt: f"hn.{k}.{lt}", sl_)
                    halo_in(gi, ub, 2, car_f[:, f, :], ubn, f"car_f.{f}")
                    ACT(lambda e: e.copy(v2(ub[:, 2:2 + GT]), PSP(b_u)), PSr(b_u), [ubn])
                    POOL(lambda e: e.tensor_copy(car_f[:, f, :], ub[:, n:n + 2]), [ubn, ubn + "h"], [f"car_f.{f}"])
                    if lastg:
                        POOL(lambda e: e.tensor_copy(ost_f[:, f, :], ub[:, n:n + 2 + NS]), [ubn, ubn + "h"], ["ost_f"])
                    conv_taps(n, lastg, ub, ucb, [vecs[:, l, V_FCW + f * 3 + k:V_FCW + f * 3 + k + 1] for k in range(3)],
                              vecs[:, l, V_FCB + f:V_FCB + f + 1],
                              [sf_in[:, f, 0, :], sf_in[:, f, 1, :]],
                              [ubn, ubn + "h", "sf_in"], ucn, wr_extra=[ucn + "h"])

                    def stage2(ucb=ucb, b_g=b_g, f=f, ucn=ucn):
                        ACT(lambda e: e.activation(ucb[:, 0:GT], ucb[:, 0:GT], AF.Silu), [ucn], [ucn, ucn + "h"])
                        DVE(lambda e: e.tensor_tensor(v2(zz[:, f, 0:GT]), v2(ucb[:, 0:GT]), PSP(b_g), ALU.mult),
                            PSr(b_g) + [ucn], [f"rg.{f}.0", f"rg.{f}.1"])
                    if pend:
                        pend.pop(0)()
                    pend.append(stage2)
            while pend:
                pend.pop(0)()
            if lastg:
                out_toks.append(S.dma("sp", lambda e: e.dma_start(out=o_f[l], in_=ost_f[:]), "o_f", reads=["ost_f"]))

            nxt = None
            if gi + 1 < len(GROUPS):
                nxt = (l, gi + 1)
            elif l + 1 < L:
                nxt = (l + 1, 0)
            if nxt is not None:
                for lt, t in enumerate(GROUPS[nxt[1]]):
                    rmsnorm(nxt[0], t, lt, V_G1)

            v6 = lambda sl: sl[:, 0:FC * 128].rearrange("p (kc j) -> p kc j", kc=FC)
            for o in range(DC):
                si = load_slot([(lambda sl: sl[:, 0:FC * 128], W6r[l, o])])
                W6 = v6(slots[si])
                if o == 0:
                    b = next_pair()
                    KS = FC - 2
                    for lt in range(2):
                        mm_group(b + lt, [W6[:, k, :] for k in range(KS)], [zz[:, k, lt * NT:(lt + 1) * NT] for k in range(KS)],
                                 [f"slot{si}"] + [f"rg.{k}.{lt}" for k in range(KS)], last=False)
                    for lt in range(2):
                        mm_group(b + lt, [W6[:, k, :] for k in range(KS, FC)],
                                 [zz[:, k, lt * NT:(lt + 1) * NT] for k in range(KS, FC)],
                                 [f"slot{si}"] + [f"rg.{k}.{lt}" for k in range(KS, FC)], first=False)
                else:
                    b = pair_mm(lambda k: W6[:, k, :], zz, lambda k, lt: f"rg.{k}.{lt}", [f"slot{si}"])
                xo = x_sb[:, o, g0:g0 + GT]
                DVE(lambda e: e.tensor_tensor(v2(xo), PSP(b), v2(xo), ALU.add),
                    PSr(b) + [f"x.{o}.{tiles[0]}", f"x.{o}.{tiles[1]}"], [f"x.{o}.{tiles[0]}", f"x.{o}.{tiles[1]}"])

            if l == L - 1:
                yT_v = yT.rearrange("(c p) n -> p c n", p=128)
                for lt, t in enumerate(tiles):
                    c0, c1 = tile_cols(t)
                    rmsnorm(l, t, lt, None, dst_is_hn=False)
                    c1r = min(c1, NTOK)
                    out_toks.append(S.dma("sp", lambda e: e.dma_start(out=yT_v[:, :, c0:c1r], in_=ystage[:, :, 0:c1r - c0]),
                                          "o_y", reads=[f"ys.{c}" for c in range(DC)] + YA_ALL))

        class _Null:
            def op(self, *a, **k):
                return ("x", 0)

            def dma(self, *a, **k):
                return ("x", 0)
        S_real = S
        plan = {"planning": True, "loads": []}
        state.update({"li": 0, "issued": 0})
        S = _Null()
        saved_bank = state["bank"]
        gen_program()
        plan["planning"] = False
        S = S_real
        state["bank"] = saved_bank
        out_toks.clear()
        gen_program()

        S.final_wait("sp", out_toks)
        S.emit()
    return nc


_NC_CACHE = {}


def _fm(v, nch):
    v = np.asarray(v)
    lead = v.shape[:-1]
    r = v.reshape(lead + (nch, 128))
    nd = r.ndim
    perm = (nd - 1, nd - 2) + tuple(range(nd - 2))
    return np.ascontiguousarray(r.transpose(perm))


def kernel(x_prompt, x_sample, state_conv_a, state_conv_b, state_rglru, state_conv_ffn,
           meta_tokens, norm_mix, norm_ffn, norm_final, w_in, b_gate, conv_a_w, w_a_out,
           conv_b_w, conv_b_b, rg_w_a, rg_b_a, rg_w_x, rg_b_x, rg_lambda, w_b_out, w_o,
           ffn_w_up, ffn_w_gate, ffn_conv_w, ffn_conv_b, ffn_w_down):
    f32 = np.float32
    x_prompt = np.asarray(x_prompt, f32)
    x_sample = np.asarray(x_sample, f32)
    ncores = 8
    vec = np.zeros((128, L, NV), f32)
    for l in range(L):
        vec[:, l, V_G1:V_G1 + 8] = _fm(np.asarray(norm_mix)[l], 8)
        vec[:, l, V_G2:V_G2 + 8] = _fm(np.asarray(norm_ffn)[l], 8)
        vec[:, l, V_BG:V_BG + 16] = _fm(np.asarray(b_gate)[l], 16)
        vec[:, l, V_CAW:V_CAW + 24] = _fm(np.asarray(conv_a_w)[l], 8).reshape(128, 24)
        vec[:, l, V_CBW:V_CBW + 32] = _fm(np.asarray(conv_b_w)[l], 8).reshape(128, 32)
        vec[:, l, V_CBB:V_CBB + 8] = _fm(np.asarray(conv_b_b)[l], 8)
        vec[:, l, V_RBA:V_RBA + 8] = _fm(np.asarray(rg_b_a)[l], 8)
        vec[:, l, V_RBX:V_RBX + 8] = _fm(np.asarray(rg_b_x)[l], 8)
        vec[:, l, V_LAM:V_LAM + 8] = _fm(np.asarray(rg_lambda)[l], 8)
        vec[:, l, V_FCW:V_FCW + 66] = _fm(np.asarray(ffn_conv_w)[l], 22).reshape(128, 66)
        vec[:, l, V_FCB:V_FCB + 22] = _fm(np.asarray(ffn_conv_b)[l], 22)
    gfin = _fm(np.asarray(norm_final), 8)
    rgw = np.zeros((L, 128, 2, 8, 128), f32)
    for wi, W in enumerate((np.asarray(rg_w_a), np.asarray(rg_w_x))):
        for c in range(8):
            for hh in range(2):
                rgw[:, hh * 64:(hh + 1) * 64, wi, c, hh * 64:(hh + 1) * 64] = W[:, 2 * c + hh]
    w_in_ = np.asarray(w_in, f32).reshape(L, DC, 128, 7, DC, 128)
    W1r = np.ascontiguousarray(w_in_[:, :, :, 0:5].transpose(0, 4, 2, 1, 3, 5)).reshape(L, DC, 128, DC * 5 * 128)
    wa_ = np.asarray(w_a_out, f32).reshape(L, DC, 128, DC, 128)
    wb_ = np.asarray(w_b_out, f32).reshape(L, DC, 128, DC, 128)
    W2r = np.empty((L, DC, 128, DC, 4, 128), f32)
    W2r[:, :, :, :, 0] = wa_.transpose(0, 3, 2, 1, 4)
    W2r[:, :, :, :, 1] = wb_.transpose(0, 3, 2, 1, 4)
    W2r[:, :, :, :, 2] = w_in_[:, :, :, 5].transpose(0, 3, 2, 1, 4)
    W2r[:, :, :, :, 3] = w_in_[:, :, :, 6].transpose(0, 3, 2, 1, 4)
    W2r = W2r.reshape(L, DC, 128, DC * 4 * 128)
    wo_ = np.asarray(w_o, f32).reshape(L, DC, 128, 2, 512)
    W3r = np.ascontiguousarray(wo_.transpose(0, 3, 2, 1, 4)).reshape(L, 2, 128, DC * 512)
    wu_ = np.asarray(ffn_w_up, f32).reshape(L, DC, 128, FC // 2, 256)
    wg_ = np.asarray(ffn_w_gate, f32).reshape(L, DC, 128, FC // 2, 256)
    W5r = np.empty((L, FC // 2, 128, DC, 2, 256), f32)
    W5r[:, :, :, :, 0] = wu_.transpose(0, 3, 2, 1, 4)
    W5r[:, :, :, :, 1] = wg_.transpose(0, 3, 2, 1, 4)
    W5r = W5r.reshape(L, FC // 2, 128, DC * 2 * 256)
    wd_ = np.asarray(ffn_w_down, f32).reshape(L, FC, 128, DC, 128)
    W6r = np.ascontiguousarray(wd_.transpose(0, 3, 2, 1, 4)).reshape(L, DC, 128, FC * 128)
    weights = {"W1r": W1r, "W2r": W2r, "W3r": W3r, "W5r": W5r, "W6r": W6r, "vecs": vec, "gfin": gfin, "rgw": rgw}
    sca = np.asarray(state_conv_a, f32)
    scb = np.asarray(state_conv_b, f32)
    srg = np.asarray(state_rglru, f32)
    scf = np.asarray(state_conv_ffn, f32)
    meta = np.asarray(meta_tokens, f32)
    in_maps = []
    for i in range(ncores):
        bs = slice(i * NS, (i + 1) * NS)
        xall = np.concatenate([meta, x_prompt[i], x_sample[bs, 0]], axis=0)
        m = dict(weights)
        m["xT"] = np.ascontiguousarray(xall.T)
        m["sta"] = np.ascontiguousarray(sca[:, bs].reshape(L, NS, 2, 8, 128).transpose(0, 4, 3, 2, 1))
        m["stb"] = np.ascontiguousarray(scb[:, bs].reshape(L, NS, 3, 8, 128).transpose(0, 4, 3, 2, 1))
        m["sth"] = np.ascontiguousarray(srg[:, bs].reshape(L, NS, 8, 128).transpose(0, 3, 2, 1))
        m["stf"] = np.ascontiguousarray(scf[:, bs].reshape(L, NS, 2, 22, 128).transpose(0, 4, 3, 2, 1))
        in_maps.append(m)

    if "nc" not in _NC_CACHE:
        _NC_CACHE["nc"] = build_nc()
    nc = _NC_CACHE["nc"]
    res = run_bass_kernel_spmd(nc, in_maps, core_ids=list(range(ncores)))
    R = res.results

    y_prompt = np.zeros((8, SEQ, D), f32)
    y_sample = np.zeros((128, 1, D), f32)
    p_a = np.zeros((L, 8, 2, D), f32)
    p_b = np.zeros((L, 8, 3, D), f32)
    p_h = np.zeros((L, 8, D), f32)
    p_f = np.zeros((L, 8, 2, DFF), f32)
    s_a = np.zeros((L, 128, 2, D), f32)
    s_b = np.zeros((L, 128, 3, D), f32)
    s_h = np.zeros((L, 128, D), f32)
    s_f = np.zeros((L, 128, 2, DFF), f32)

    def tm(a):
        Lh, P, nch, r = a.shape
        return a.transpose(0, 3, 2, 1).reshape(Lh, r, nch * P)

    for i in range(ncores):
        r = R[i]
        bs = slice(i * NS, (i + 1) * NS)
        yTi = np.asarray(r["yT"])
        y_prompt[i] = yTi[:, NMETA:NMETA + SEQ].T
        y_sample[bs, 0] = yTi[:, NMETA + SEQ:].T
        oa = tm(np.asarray(r["o_a"]))
        ob = tm(np.asarray(r["o_b"]))
        oh = tm(np.asarray(r["o_h"]))
        of = tm(np.asarray(r["o_f"]))
        p_a[:, i] = oa[:, 0:2]
        s_a[:, bs, 1] = oa[:, 2:18]
        p_b[:, i] = ob[:, 0:3]
        s_b[:, bs, 2] = ob[:, 3:19]
        p_h[:, i] = oh[:, 0]
        s_h[:, bs] = oh[:, 1:17]
        p_f[:, i] = of[:, 0:2]
        s_f[:, bs, 1] = of[:, 2:18]
        s_a[:, bs, 0] = tm(np.asarray(r["sh_a"]))
        shb = np.asarray(r["sh_b"])
        for k in range(2):
            s_b[:, bs, k] = tm(np.ascontiguousarray(shb[:, :, :, k, :]))
        s_f[:, bs, 0] = tm(np.asarray(r["sh_f"]))
    return (y_prompt, y_sample, p_a, p_b, p_h, p_f, s_a, s_b, s_h, s_f)
```

```python
from contextlib import ExitStack
import types
import numpy as np
import concourse.bass as bass
import concourse.mybir as mybir
from concourse.bass_utils import run_bass_kernel_spmd

F32 = mybir.dt.float32
BF16 = mybir.dt.bfloat16
AF = mybir.ActivationFunctionType
ALU = mybir.AluOpType

L = 4
D = 1024
DC = 8
DFF = 2816
FC = 22
DIN = 7168
NMETA = 16
SEQ = 2048
NS = 16
NTOK = NMETA + SEQ + NS
NT = 352
NTILES = 6
NPL = 304
SE = NPL + NS
NCOL = NTILES * NT
EPS = 1e-6
GROUPS = [[0, 1], [2, 3], [4, 5]]
GT = max(len(g) for g in GROUPS) * NT
NV = 208
SLOT = 5120
NSLOT = 3

ENGS = ("pe", "act", "dve", "pool", "sp")


def _freeze(fn):
    if fn is None or fn.__closure__ is None:
        return fn
    cells = []
    for c in fn.__closure__:
        try:
            cells.append(types.CellType(c.cell_contents))
        except ValueError:
            cells.append(c)
    return types.FunctionType(fn.__code__, fn.__globals__, fn.__name__, fn.__defaults__, tuple(cells))


class Sched:
    def __init__(self, nc, stack, same_engine_sync=True):
        self.nc = nc
        self.stack = stack
        self.same = same_engine_sync
        self.ops = {e: [] for e in ENGS}
        self.cnt = {}
        self.sems = {}
        self.seen = {e: {} for e in ENGS}
        self.lastw = {}
        self.readers = {}
        for e in ENGS:
            self._sem("E_" + e)

    def _sem(self, key):
        if key not in self.sems:
            self.sems[key] = self.stack.enter_context(self.nc.semaphore(key))
            self.cnt[key] = 0
        return self.sems[key]

    def _deps(self, reads, writes):
        deps = {}

        def add(tok):
            if tok is None:
                return
            k, v = tok
            if deps.get(k, 0) < v:
                deps[k] = v

        for r in reads:
            add(self.lastw.get(r))
        for w in writes:
            add(self.lastw.get(w))
            for t in self.readers.get(w, ()):
                add(t)
        return deps

    def _waits(self, eng, deps):
        waits = []
        seen = self.seen[eng]
        own = "E_" + eng
        for k, v in deps.items():
            if k == own and not self.same:
                continue
            if seen.get(k, 0) < v:
                seen[k] = v
                waits.append((k, v))
        return waits

    def _commit(self, tok, reads, writes):
        for r in reads:
            self.readers.setdefault(r, []).append(tok)
        for w in writes:
            self.lastw[w] = tok
            self.readers[w] = []

    def op(self, eng, fn, reads=(), writes=()):
        deps = self._deps(reads, writes)
        waits = self._waits(eng, deps)
        key = "E_" + eng
        self.cnt[key] += 1
        tok = (key, self.cnt[key])
        self.ops[eng].append((waits, _freeze(fn), key, 1))
        self._commit(tok, reads, writes)
        return tok

    def dma(self, eng, fn, semname, reads=(), writes=()):
        key = "D_" + semname
        self._sem(key)
        deps = self._deps(reads, writes)
        waits = self._waits(eng, deps)
        self.cnt[key] += 16
        tok = (key, self.cnt[key])
        self.ops[eng].append((waits, _freeze(fn), key, 16))
        self._commit(tok, reads, writes)
        return tok

    def final_wait(self, eng, toks):
        deps = {}
        for k, v in toks:
            if deps.get(k, 0) < v:
                deps[k] = v
        waits = self._waits(eng, deps)
        self.ops[eng].append((waits, None, None, 0))

    def emit(self):
        nc = self.nc
        S = self
        with nc.Block() as block:
            def run(e, h):
                for waits, fn, key, inc in S.ops[e]:
                    for k, v in waits:
                        h.wait_ge(S.sems[k], v)
                    if fn is not None:
                        ins = fn(h)
                        ins.then_inc(S.sems[key], inc)

            @block.tensor
            def _(h):
                run("pe", h)

            @block.scalar
            def _(h):
                run("act", h)

            @block.vector
            def _(h):
                run("dve", h)

            @block.gpsimd
            def _(h):
                run("pool", h)

            @block.sync
            def _(h):
                run("sp", h)


V_G1, V_G2, V_BG, V_CAW, V_CBW, V_CBB, V_RBA, V_RBX, V_LAM, V_FCW, V_FCB = (
    0, 8, 16, 32, 56, 88, 96, 104, 112, 120, 186)


def build_nc(debug=False):
    nc = bass.Bass("TRN2", target_bir_lowering=False)

    def din(name, shape):
        return nc.dram_tensor(name, list(shape), F32, kind="ExternalInput").ap()

    def dout(name, shape):
        return nc.dram_tensor(name, list(shape), F32, kind="ExternalOutput").ap()

    xT = din("xT", [D, NTOK])
    vecs_d = din("vecs", [128, L, NV])
    gfin_d = din("gfin", [128, DC])
    sta_d = din("sta", [L, 128, DC, 2, NS])
    stb_d = din("stb", [L, 128, DC, 3, NS])
    sth_d = din("sth", [L, 128, DC, NS])
    stf_d = din("stf", [L, 128, FC, 2, NS])
    rgw_d = din("rgw", [L, 128, 2, DC, 128])
    W1r = din("W1r", [L, DC, 128, DC * 5 * 128])
    W2r = din("W2r", [L, DC, 128, DC * 4 * 128])
    W3r = din("W3r", [L, 2, 128, DC * 512])
    W5r = din("W5r", [L, FC // 2, 128, DC * 2 * 256])
    W6r = din("W6r", [L, DC, 128, FC * 128])

    yT = dout("yT", [D, NTOK])
    o_a = dout("o_a", [L, 128, DC, 18])
    o_b = dout("o_b", [L, 128, DC, 19])
    o_h = dout("o_h", [L, 128, DC, 17])
    o_f = dout("o_f", [L, 128, FC, 18])
    sh_a = dout("sh_a", [L, 128, DC, NS])
    sh_b = dout("sh_b", [L, 128, DC, 2, NS])
    sh_f = dout("sh_f", [L, 128, FC, NS])

    dbg = dout("dbg", [8, 128, NT]) if debug else None

    with ExitStack() as st:
        def sbt(name, shape, dt):
            return st.enter_context(nc.sbuf_tensor(name, list(shape), dt))

        x_sb = sbt("x_sb", [128, DC, NCOL], F32)
        hn = sbt("hn", [128, DC, GT], BF16)
        region = sbt("region", [128, 24 * GT], BF16)
        ya = region[:, 0:8 * GT].rearrange("p (c n) -> p c n", c=DC)
        yb = region[:, 8 * GT:16 * GT].rearrange("p (c n) -> p c n", c=DC)
        mixed = region[:, 16 * GT:24 * GT].rearrange("p (c n) -> p c n", c=DC)
        zz = region[:, 0:FC * GT].rearrange("p (c n) -> p c n", c=FC)
        ystage = region[:, 0:2 * DC * NT].bitcast(F32).rearrange("p (c n) -> p c n", c=DC)
        sq = sbt("sq", [128, DC, NT], BF16)
        rstd = sbt("rstd", [128, NT], F32)
        slots = [sbt(f"slot{i}", [128, SLOT], BF16) for i in range(NSLOT)]
        rgw = sbt("rgw_sb", [128, 2, DC, 128], BF16)
        vecs = sbt("vecs_sb", [128, L, NV], F32)
        gfin = sbt("gfin_sb", [128, DC], F32)
        sc1 = sbt("sc1", [128, L, DC], F32)
        sc05 = sbt("sc05", [128, L, DC], F32)
        hba = sbt("hba", [128, L, DC], F32)
        hbx = sbt("hbx", [128, L, DC], F32)
        tmpv = [sbt(f"tmpv{i}", [128, L, DC], F32) for i in range(6)]
        sa_in = sbt("sa_in", [128, DC, 2, NS], F32)
        sb_in = sbt("sb_in", [128, DC, 3, NS], F32)
        sh_in = sbt("sh_in", [128, DC, NS], F32)
        sf_in = sbt("sf_in", [128, FC, 2, NS], F32)
        car_a = sbt("car_a", [128, DC, 2], F32)
        car_b = sbt("car_b", [128, DC, 3], F32)
        car_h = sbt("car_h", [128, DC], F32)
        car_f = sbt("car_f", [128, FC, 2], F32)
        ost_a = sbt("ost_a", [128, DC, 18], F32)
        ost_b = sbt("ost_b", [128, DC, 19], F32)
        ost_h = sbt("ost_h", [128, DC, 17], F32)
        ost_f = sbt("ost_f", [128, FC, 18], F32)
        ones_bf = sbt("ones_bf", [128, 128], BF16)
        epsv = sbt("epsv", [128, 1], F32)
        onev = sbt("onev", [128, 1], F32)
        NWB = 8
        WB = [sbt(f"wb{k}", [128, GT + 4], F32) for k in range(NWB)]
        xcb = sbt("xcb", [128, GT], BF16)
        WBG = sbt("wbg", [128, GT + 4], F32)
        WBA = sbt("wba", [128, GT + 4], F32)
        WBB = sbt("wbb", [128, GT + 4], F32)
        Hb = [sbt(f"H{p}", [128, GT], F32) for p in range(2)]
        psa = st.enter_context(nc.psum_tensor("psa", [128, 8, 512], F32))

        def PS(b, n=NT):
            return psa[:, b, 0:n]

        def PSP(b):
            return psa[:, b:b + 2, 0:NT]

        S = Sched(nc, st)
        state = {"bank": 0, "slot": 0, "unit": 0}

        def next_bank():
            b = state["bank"]
            state["bank"] = (b + 1) % 8
            return b

        def next_pair():
            b = state["bank"]
            if b % 2:
                b = (b + 1) % 8
            state["bank"] = (b + 2) % 8
            return b

        def next_slot():
            s = state["slot"]
            state["slot"] = (s + 1) % NSLOT
            return s

        def ACT(fn, r, w):
            return S.op("act", fn, r, w)

        def DVE(fn, r, w):
            return S.op("dve", fn, r, w)

        out_toks = []

        S.dma("sp", lambda e: e.dma_start(out=vecs[:], in_=vecs_d), "vecs", writes=["vecs"])
        S.dma("sp", lambda e: e.dma_start(out=gfin[:], in_=gfin_d), "gfin", writes=["gfin"])
        xT_v = xT.rearrange("(c p) n -> p c n", p=128)
        for t in range(NTILES):
            hi = min((t + 1) * NT, NTOK)
            if t == 2:
                deferred_x = []
            fnx = (lambda e, t=t, hi=hi: e.dma_start(out=x_sb[:, :, t * NT:hi], in_=xT_v[:, :, t * NT:hi]))
            if t < 2:
                S.dma("sp", fnx, f"xin{t}", writes=[f"x.{c}.{t}" for c in range(DC)])
            else:
                deferred_x.append((fnx, t))
        for c in range(DC):
            DVE(lambda e, c=c: e.memset(x_sb[:, c, NTOK:NCOL], 0.0), [], [f"x.{c}.{NTILES - 1}"])
        DVE(lambda e: e.memset(ones_bf[:], 1.0 / D), [], ["ones"])
        DVE(lambda e: e.memset(epsv[:], EPS), [], ["epsv"])
        DVE(lambda e: e.memset(onev[:], 1.0), [], ["onev"])
        lam_v = vecs[:, :, V_LAM:V_LAM + DC]
        DVE(lambda e: e.tensor_scalar(tmpv[0][:], lam_v, -1.0, None, ALU.mult), ["vecs"], ["tv0"])
        ACT(lambda e: e.activation(tmpv[1][:], tmpv[0][:], AF.Abs), ["tv0"], ["tv1"])
        ACT(lambda e: e.activation(tmpv[1][:], tmpv[1][:], AF.Exp, scale=-1.0), ["tv1"], ["tv1"])
        tu, tu2, tp = tmpv[3], tmpv[4], tmpv[5]
        DVE(lambda e: e.tensor_scalar(tu[:], tmpv[1][:], 2.0, None, ALU.add), ["tv1"], ["tu"])
        DVE(lambda e: e.reciprocal(tu[:], tu[:]), ["tu"], ["tu"])
        DVE(lambda e: e.tensor_tensor(tu[:], tu[:], tmpv[1][:], ALU.mult), ["tu", "tv1"], ["tu"])
        DVE(lambda e: e.tensor_tensor(tu2[:], tu[:], tu[:], ALU.mult), ["tu"], ["tu2"])
        DVE(lambda e: e.tensor_scalar(tp[:], tu2[:], 1.0 / 11.0, 1.0 / 9.0, ALU.mult, ALU.add), ["tu2"], ["tp"])
        for coef in (1.0 / 7.0, 1.0 / 5.0, 1.0 / 3.0, 1.0):
            DVE(lambda e: e.tensor_tensor(tp[:], tp[:], tu2[:], ALU.mult), ["tp", "tu2"], ["tp"])
            DVE(lambda e, coef=coef: e.tensor_scalar(tp[:], tp[:], coef, None, ALU.add), ["tp"], ["tp"])
        DVE(lambda e: e.scalar_tensor_tensor(tmpv[1][:], tp[:], 2.0, tu[:], ALU.mult, ALU.mult), ["tp", "tu"], ["tv1"])
        DVE(lambda e: e.tensor_scalar(tmpv[2][:], tmpv[0][:], 0.0, None, ALU.max), ["tv0"], ["tv2"])
        DVE(lambda e: e.tensor_tensor(tmpv[2][:], tmpv[2][:], tmpv[1][:], ALU.add), ["tv1", "tv2"], ["tv2"])
        DVE(lambda e: e.tensor_scalar(sc1[:], tmpv[2][:], -8.0, None, ALU.mult), ["tv2"], ["sc1"])
        DVE(lambda e: e.tensor_scalar(sc05[:], tmpv[2][:], -4.0, None, ALU.mult), ["tv2"], ["sc05"])
        DVE(lambda e: e.tensor_scalar(hba[:], vecs[:, :, V_RBA:V_RBA + DC], 0.5, None, ALU.mult), ["vecs"], ["hba"])
        DVE(lambda e: e.tensor_scalar(hbx[:], vecs[:, :, V_RBX:V_RBX + DC], 0.5, None, ALU.mult), ["vecs"], ["hbx"])
        VEC = ["vecs", "sc1", "sc05", "hba", "hbx", "gfin"]
        YA_ALL = [f"rg.{k}.{q}" for k in range(DC) for q in range(GT // NT)]

        def tile_cols(t):
            return t * NT, (t + 1) * NT

        def npr(t):
            return NPL if t == NTILES - 1 else NT

        def POOL(fn, r, w):
            return S.op("pool", fn, r, w)

        def mm_group(bank, lhs_list, rhs_list, reads, n=NT, first=True, last=True):
            def fn(e):
                ins = None
                nk = len(lhs_list)
                for k in range(nk):
                    ins = e.matmul(PS(bank, n), lhs_list[k], rhs_list[k],
                                   start=(first and k == 0), stop=(last and k == nk - 1))
                return ins
            if not plan["planning"] and first:
                key = f"ps{bank}"
                assert not (key in S.lastw and not S.readers.get(key)), \
                    f"PSUM bank {bank} overwritten before its consumer was recorded"
            S.op("pe", fn, reads, [f"ps{bank}"])

        def norm_sq(t, c_lo, c_hi):
            c0, c1 = tile_cols(t)
            ACT(lambda e: e.activation(sq[:, c_lo:c_hi, :], x_sb[:, c_lo:c_hi, c0:c1], AF.Square),
                [f"x.{c}.{t}" for c in range(c_lo, c_hi)], [f"sq.{c}" for c in range(c_lo, c_hi)])

        def norm_stats():
            b = next_bank()
            mm_group(b, [ones_bf[:]] * DC, [sq[:, c, :] for c in range(DC)], ["ones"] + [f"sq.{c}" for c in range(DC)])
            ACT(lambda e: e.activation(rstd[:], PS(b), AF.Ln, bias=epsv[:, 0:1]), [f"ps{b}", "epsv"], ["rstd"])
            ACT(lambda e: e.activation(rstd[:], rstd[:], AF.Exp, scale=-0.5), ["rstd"], ["rstd"])

        def norm_scale(l, t, lt, gcol, dst_is_hn=True):
            c0, c1 = tile_cols(t)
            for c in range(DC):
                if dst_is_hn:
                    dst = hn[:, c, lt * NT:(lt + 1) * NT]
                    wr = [f"hn.{c}.{lt}"]
                    g = vecs[:, l, gcol + c:gcol + c + 1]
                else:
                    dst = ystage[:, c, :]
                    wr = [f"ys.{c}"] + YA_ALL
                    g = gfin[:, c:c + 1]
                DVE(lambda e: e.scalar_tensor_tensor(
                    dst, x_sb[:, c, c0:c1], g, rstd[:], ALU.mult, ALU.mult),
                    [f"x.{c}.{t}", "rstd"] + VEC, wr)

        def rmsnorm(l, t, lt, gcol, dst_is_hn=True):
            norm_sq(t, 0, DC)
            norm_stats()
            norm_scale(l, t, lt, gcol, dst_is_hn)

        LA = NSLOT - 1

        def issue_load(j):
            si = j % NSLOT
            for (vf, src) in plan["loads"][j]:
                S.dma("pool", lambda e: e.dma_start(out=vf(slots[si]), in_=src),
                      f"slot{si}", writes=[f"slot{si}"])

        def load_slot(dmas, hold=0):
            if plan["planning"]:
                plan["loads"].append(dmas)
                return (len(plan["loads"]) - 1) % NSLOT
            i = state["li"]
            state["li"] += 1
            while state["issued"] < min(len(plan["loads"]), i + LA + 1 - hold):
                issue_load(state["issued"])
                state["issued"] += 1
            return i % NSLOT

        def conv_taps(n, lastg, src, dst, taps, bias, sample_bufs, rd, wr_name, which="all", wr_extra=()):
            W = len(taps)
            segs = [(0, n, None)]
            if lastg:
                segs.append((n, n + NS, sample_bufs))
            for (a, b_, sb_) in segs:
                def inp(k):
                    if sb_ is None or k == W - 1:
                        return src[:, a + k:b_ + k]
                    return sb_[k]
                i0 = inp(0)
                if which in ("all", "first"):
                    if bias is not None:
                        ACT(lambda e: e.activation(dst[:, a:b_], i0, AF.Identity, bias=bias, scale=taps[0]),
                            rd + VEC, [wr_name] + list(wr_extra))
                    else:
                        ACT(lambda e: e.activation(dst[:, a:b_], i0, AF.Identity, scale=taps[0]),
                            rd + VEC, [wr_name] + list(wr_extra))
                if which in ("all", "rest"):
                    for k in range(1, W):
                        ik = inp(k)
                        DVE(lambda e: e.scalar_tensor_tensor(
                            dst[:, a:b_], ik, taps[k], dst[:, a:b_], ALU.mult, ALU.add),
                            rd + VEC + [wr_name], [wr_name] + list(wr_extra))

        def halo_in(gi, buf, hw, car, resn, carn):
            if gi == 0:
                POOL(lambda e: e.memset(buf[:, 0:hw], 0.0), [], [resn + "h"])
            else:
                POOL(lambda e: e.tensor_copy(buf[:, 0:hw], car), [carn], [resn + "h"])

        def gen_program():
            for l in range(L):
                if not plan["planning"]:
                    gen_layer_prologue(l)
                for gi, tiles in enumerate(GROUPS):
                    gen_group(l, gi, tiles)

        def gen_layer_prologue(l):
            S.dma("sp", lambda e: e.dma_start(out=sa_in[:], in_=sta_d[l]), "st_sa_in", writes=["sa_in"])
            S.dma("sp", lambda e: e.dma_start(out=sb_in[:], in_=stb_d[l]), "st_sb_in", writes=["sb_in"])
            S.dma("sp", lambda e: e.dma_start(out=sh_in[:], in_=sth_d[l]), "st_sh_in", writes=["sh_in"])
            S.dma("sp", lambda e: e.dma_start(out=sf_in[:], in_=stf_d[l]), "st_sf_in", writes=["sf_in"])
            S.dma("pool", lambda e: e.dma_start(out=rgw[:], in_=rgw_d[l]), "rgw", writes=["rgw"])
            out_toks.append(S.dma("sp", lambda e: e.dma_start(out=sh_a[l], in_=sa_in[:, :, 1, :]),
                                  "o_sha", reads=["sa_in"]))
            out_toks.append(S.dma("sp", lambda e: e.dma_start(out=sh_b[l], in_=sb_in[:, :, 1:3, :]),
                                  "o_shb", reads=["sb_in"]))
            out_toks.append(S.dma("sp", lambda e: e.dma_start(out=sh_f[l], in_=sf_in[:, :, 1, :]),
                                  "o_shf", reads=["sf_in"]))

        def gen_group(l, gi, tiles):
            ntl = len(tiles)
            assert ntl == 2
            lastg = (gi == len(GROUPS) - 1)
            n = (NT + NPL) if lastg else GT
            SEg = n + NS
            g0 = gi * GT
            hn_r = lambda lt: [f"hn.{k}.{lt}" for k in range(DC)]

            def v2(ap2d):
                return ap2d.rearrange("p (t n) -> p t n", t=2)

            def pair_mm(lhs_fn, src, names, extra, pair=None):
                b = next_pair() if pair is None else 2 * (pair % 4)
                nk = src.shape[1]
                for lt in range(2):
                    mm_group(b + lt, [lhs_fn(k) for k in range(nk)],
                             [src[:, k, lt * NT:(lt + 1) * NT] for k in range(nk)],
                             extra + [names(k, lt) for k in range(nk)])
                return b

            def PSr(b):
                return [f"ps{b}", f"ps{b + 1}"]

            if l == 0 and gi == 0:
                for lt, t in enumerate(tiles):
                    rmsnorm(l, t, lt, V_G1)
                if not plan["planning"]:
                    for (fnx, t) in deferred_x:
                        S.dma("sp", fnx, f"xin{t}", reads=[f"hn.0.1"], writes=[f"x.{c}.{t}" for c in range(DC)])

            v1 = lambda sl: sl[:, 0:DC * 5 * 128].rearrange("p (kc s j) -> p kc s j", kc=DC, s=5)
            gCs, ua, xr, xc, Ab, Mb, Bb = WB[0:7]
            Gbs = [WB[7], WBG]
            hnn = lambda k, lt: f"hn.{k}.{lt}"
            chunk = {}

            def pe1(c):
                si = load_slot([(lambda sl: sl[:, 0:DC * 5 * 128], W1r[l, c])])
                W1 = v1(slots[si])
                sl_ = [f"slot{si}"]
                base = pb0 + c
                bg = pair_mm(lambda k: W1[:, k, 4, :], hn, hnn, sl_, pair=base)
                bx = pair_mm(lambda k: W1[:, k, 3, :], hn, hnn, sl_, pair=base + 1)
                chunk[c] = {"W1": W1, "sl": sl_, "bg": bg, "bx": bx, "base": base}

            def pe2(c):
                W1, sl_ = chunk[c]["W1"], chunk[c]["sl"]
                base = chunk[c]["base"]
                chunk[c]["bc"] = pair_mm(lambda k: W1[:, k, 1, :], hn, hnn, sl_, pair=base + 2)
                chunk[c]["bh"] = pair_mm(lambda k: W1[:, k, 2, :], hn, hnn, sl_, pair=base + 3)
                chunk[c]["bb"] = pair_mm(lambda k: W1[:, k, 0, :], hn, hnn, sl_, pair=base)

            def front_a(c):
                ck = chunk[c]
                bg, bx = ck["bg"], ck["bx"]
                Gb = Gbs[c % 2]
                Gn = ("wb7" if c % 2 == 0 else "wbg")
                ACT(lambda e: e.activation(v2(Gb[:, 0:GT]), PSP(bg), AF.Gelu_apprx_tanh), PSr(bg), [Gn, Gn + "h"])
                halo_in(gi, xr, 3, car_b[:, c, :], "wb2", f"car_b.{c}")
                ACT(lambda e: e.copy(v2(xr[:, 3:3 + GT]), PSP(bx)), PSr(bx), ["wb2"])
                POOL(lambda e: e.tensor_copy(car_b[:, c, :], xr[:, n:n + 3]), ["wb2", "wb2h"], [f"car_b.{c}"])
                if lastg:
                    POOL(lambda e: e.tensor_copy(ost_b[:, c, :], xr[:, n:n + 3 + NS]), ["wb2", "wb2h"], ["ost_b"])

            def front_b(c):
                ck = chunk[c]
                bc, bh, bb = ck["bc"], ck["bh"], ck["bb"]
                ACT(lambda e: e.copy(v2(gCs[:, 0:GT]), PSP(bc)), PSr(bc), ["wb0", "wb0h"])
                halo_in(gi, ua, 2, car_a[:, c, :], "wb1", f"car_a.{c}")
                DVE(lambda e: e.tensor_tensor(v2(ua[:, 2:2 + GT]), v2(gCs[:, 0:GT]), PSP(bh), ALU.mult),
                    PSr(bh) + ["wb0", "wb0h"], ["wb1"])
                POOL(lambda e: e.tensor_copy(car_a[:, c, :], ua[:, n:n + 2]), ["wb1", "wb1h"], [f"car_a.{c}"])
                if lastg:
                    POOL(lambda e: e.tensor_copy(ost_a[:, c, :], ua[:, n:n + 2 + NS]), ["wb1", "wb1h"], ["ost_a"])
                ca = gCs
                tb = [vecs[:, l, V_CBW + c * 4 + k:V_CBW + c * 4 + k + 1] for k in range(4)]
                ta = [vecs[:, l, V_CAW + c * 3 + k:V_CAW + c * 3 + k + 1] for k in range(3)]
                sbb = [sb_in[:, c, 0, :], sb_in[:, c, 1, :], sb_in[:, c, 2, :]]
                sba = [sa_in[:, c, 0, :], sa_in[:, c, 1, :]]
                conv_taps(n, lastg, xr, xc, tb, vecs[:, l, V_CBB + c:V_CBB + c + 1], sbb,
                          ["wb2", "wb2h", "sb_in"], "wb3", which="first", wr_extra=["wb3h"])
                conv_taps(n, lastg, ua, ca, ta, None, sba, ["wb1", "wb1h", "sa_in"], "wb0", which="first", wr_extra=["wb0h"])
                conv_taps(n, lastg, xr, xc, tb, vecs[:, l, V_CBB + c:V_CBB + c + 1], sbb,
                          ["wb2", "wb2h", "sb_in"], "wb3", which="rest", wr_extra=["wb3h"])
                ACT(lambda e: e.copy(xcb[:, 0:GT], xc[:, 0:GT]), ["wb3", "wb3h"], ["xcb"])
                conv_taps(n, lastg, ua, ca, ta, None, sba, ["wb1", "wb1h", "sa_in"], "wb0", which="rest", wr_extra=["wb0h"])
                DVE(lambda e: e.tensor_tensor(v2(ya[:, c, 0:GT]), PSP(bb), v2(ca[:, 0:GT]), ALU.mult),
                    PSr(bb) + ["wb0", "wb0h"], [f"rg.{c}.0", f"rg.{c}.1"])

            def gates_pe(c):
                base = chunk[c]["base"]
                br = 2 * ((base + 3) % 4)
                for lt in range(2):
                    mm_group(br + lt, [rgw[:, 0, c, :]], [xcb[:, lt * NT:(lt + 1) * NT]], ["rgw", "xcb"])
                bi = 2 * (base % 4)
                for lt in range(2):
                    mm_group(bi + lt, [rgw[:, 1, c, :]], [xcb[:, lt * NT:(lt + 1) * NT]], ["rgw", "xcb"])
                chunk[c]["br"], chunk[c]["bi"] = br, bi

            def tail_elem(c, part):
                br, bi = chunk[c]["br"], chunk[c]["bi"]
                Hc = Hb[c % 2]
                Hn = f"H{c % 2}"
                Gb = Gbs[c % 2]
                Gn = ("wb7" if c % 2 == 0 else "wbg")
                Ab = (WB[4], WBA)[c % 2]
                Bb = (WB[6], WBB)[c % 2]
                An = ("wb4", "wba")[c % 2]
                Bn = ("wb6", "wbb")[c % 2]
                AR = [An, An + "h"]
                BR = [Bn, Bn + "h"]
                if part == 1:
                    ACT(lambda e: e.activation(v2(Ab[:, 0:GT]), PSP(br), AF.Tanh, bias=hba[:, l, c:c + 1], scale=0.5),
                        PSr(br) + VEC, AR)
                    ACT(lambda e: e.activation(Ab[:, 0:GT], Ab[:, 0:GT], AF.Exp,
                                               bias=sc05[:, l, c:c + 1], scale=sc05[:, l, c:c + 1]),
                        AR + VEC, AR)
                    ACT(lambda e: e.activation(v2(Bb[:, 0:GT]), PSP(bi), AF.Tanh, bias=hbx[:, l, c:c + 1], scale=0.5),
                        PSr(bi) + VEC, BR)
                    DVE(lambda e: e.tensor_tensor(Mb[:, 0:GT], Ab[:, 0:GT], Ab[:, 0:GT], ALU.mult), AR, ["wb5", "wb5h"])
                    DVE(lambda e: e.scalar_tensor_tensor(Bb[:, 0:GT], Bb[:, 0:GT], 1.0, xc[:, 0:GT], ALU.add, ALU.mult),
                        BR + ["wb3", "wb3h"], BR)
                else:
                    ACT(lambda e: e.activation(Mb[:, 0:GT], Mb[:, 0:GT], AF.Sqrt, bias=onev[:, 0:1], scale=-1.0),
                        ["wb5", "wb5h", "onev"], ["wb5", "wb5h"])
                    if gi == 0:
                        POOL(lambda e: e.memset(Mb[:, 0:1], 1.0), ["wb5", "wb5h"], ["wb5", "wb5h"])
                    DVE(lambda e: e.scalar_tensor_tensor(Bb[:, 0:GT], Bb[:, 0:GT], 0.5, Mb[:, 0:GT], ALU.mult, ALU.mult),
                        BR + ["wb5", "wb5h"], BR)
                    if gi == 0:
                        DVE(lambda e: e.tensor_tensor_scan(Hc[:, 0:n], Ab[:, 0:n], Bb[:, 0:n], 0.0, ALU.mult, ALU.add),
                            AR + BR, [Hn])
                    else:
                        DVE(lambda e: e.tensor_tensor_scan(Hc[:, 0:n], Ab[:, 0:n], Bb[:, 0:n], car_h[:, c:c + 1],
                                                           ALU.mult, ALU.add),
                            AR + BR + [f"car_h.{c}"], [Hn])
                    POOL(lambda e: e.tensor_copy(car_h[:, c:c + 1], Hc[:, n - 1:n]), [Hn], [f"car_h.{c}"])
                    if lastg:
                        hs_ = Hc[:, n:SEg]
                        POOL(lambda e: e.tensor_tensor(hs_, Ab[:, n:SEg], sh_in[:, c, :], ALU.mult),
                             AR + ["sh_in"], [Hn + "s"])
                        POOL(lambda e: e.tensor_tensor(hs_, hs_, Bb[:, n:SEg], ALU.add), BR + [Hn + "s"], [Hn + "s"])
                        POOL(lambda e: e.tensor_copy(ost_h[:, c, :], Hc[:, n - 1:SEg]), [Hn, Hn + "s"], ["ost_h"])
                    DVE(lambda e: e.tensor_tensor(yb[:, c, 0:GT], Gb[:, 0:GT], Hc[:, 0:GT], ALU.mult),
                        [Gn, Gn + "h", Hn, Hn + "s"], [f"rg.{8 + c}.0", f"rg.{8 + c}.1"])
                    if debug and (not plan["planning"]) and l == 0 and c == 0 and gi == 0:
                        for di, (bufn, resn) in enumerate(((Ab, An), (Mb, "wb5"), (Bb, Bn), (xc, "wb3"), (Hc, Hn), (xr, "wb2"))):
                            out_toks.append(S.dma("sp", lambda e: e.dma_start(out=dbg[di], in_=bufn[:, 0:NT]),
                                                  f"dbg{di}", reads=[resn]))

            pb0 = next_pair() // 2
            pe1(0)
            front_a(0)
            pe2(0)
            front_b(0)
            for c in range(DC):
                if c + 1 < DC:
                    pe1(c + 1)
                gates_pe(c)
                tail_elem(c, 1)
                if c + 1 < DC:
                    front_a(c + 1)
                    pe2(c + 1)
                    front_b(c + 1)
                tail_elem(c, 2)

            if lastg:
                out_toks.append(S.dma("sp", lambda e: e.dma_start(out=o_a[l], in_=ost_a[:]), "o_a", reads=["ost_a"]))
                out_toks.append(S.dma("sp", lambda e: e.dma_start(out=o_b[l], in_=ost_b[:]), "o_b", reads=["ost_b"]))
                out_toks.append(S.dma("sp", lambda e: e.dma_start(out=o_h[l], in_=ost_h[:]), "o_h", reads=["ost_h"]))

            v4 = lambda sl: sl[:, 0:DC * 4 * 128].rearrange("p (kc s j) -> p kc s j", kc=DC, s=4)
            p2 = {}

            def p2_front(m, hold):
                si = load_slot([(lambda sl: sl[:, 0:DC * 4 * 128], W2r[l, m])], hold=hold)
                W2 = v4(slots[si])
                sl_ = [f"slot{si}"]
                q = m % 2
                sa_b, sb_b = WB[2 * q], WB[2 * q + 1]
                sa_n, sb_n = f"wb{2 * q}", f"wb{2 * q + 1}"
                b_ma = pair_mm(lambda k: W2[:, k, 2, :], hn, lambda k, lt: f"hn.{k}.{lt}", sl_)
                b_mb = pair_mm(lambda k: W2[:, k, 3, :], hn, lambda k, lt: f"hn.{k}.{lt}", sl_)
                ACT(lambda e: e.activation(v2(sa_b[:, 0:GT]), PSP(b_ma), AF.Sigmoid,
                                           bias=vecs[:, l, V_BG + m:V_BG + m + 1]),
                    PSr(b_ma) + VEC, [sa_n, sa_n + "h"])
                ACT(lambda e: e.activation(v2(sb_b[:, 0:GT]), PSP(b_mb), AF.Sigmoid,
                                           bias=vecs[:, l, V_BG + 8 + m:V_BG + 8 + m + 1]),
                    PSr(b_mb) + VEC, [sb_n, sb_n + "h"])
                p2[m] = (W2, sl_, sa_b, sb_b, sa_n, sb_n)

            def p2_back(m):
                W2, sl_, sa_b, sb_b, sa_n, sb_n = p2[m]
                b_ya = pair_mm(lambda k: W2[:, k, 0, :], ya, lambda k, lt: f"rg.{k}.{lt}", sl_)
                b_yb = pair_mm(lambda k: W2[:, k, 1, :], yb, lambda k, lt: f"rg.{8 + k}.{lt}", sl_)
                DVE(lambda e: e.tensor_tensor(v2(sa_b[:, 0:GT]), v2(sa_b[:, 0:GT]), PSP(b_ya), ALU.mult),
                    PSr(b_ya) + [sa_n], [sa_n, sa_n + "h"])
                DVE(lambda e: e.tensor_tensor(v2(sb_b[:, 0:GT]), v2(sb_b[:, 0:GT]), PSP(b_yb), ALU.mult),
                    PSr(b_yb) + [sb_n], [sb_n, sb_n + "h"])
                DVE(lambda e: e.tensor_tensor(mixed[:, m, 0:GT], sa_b[:, 0:GT], sb_b[:, 0:GT], ALU.add),
                    [sa_n, sb_n], [f"rg.{16 + m}.0", f"rg.{16 + m}.1"])

            p2_front(0, 0)
            for m in range(DC):
                if m + 1 < DC:
                    p2_front(m + 1, 1)
                p2_back(m)

            v3 = lambda sl: sl[:, 0:DC * 512].rearrange("p (kc j) -> p kc j", kc=DC)
            sis = [load_slot([(lambda sl: sl[:, 0:DC * 512], W3r[l, half])], hold=half) for half in range(2)]
            for lt, t in enumerate(tiles):
                cs = slice(lt * NT, (lt + 1) * NT)
                c0, c1 = tile_cols(t)
                for hf_ in range(2):
                    evacs = []
                    for o in range(hf_ * 4, hf_ * 4 + 4):
                        W3 = v3(slots[sis[o // 4]])
                        oo = o % 4
                        b = next_bank()
                        mm_group(b, [W3[:, k, oo * 128:(oo + 1) * 128] for k in range(DC)],
                                 [mixed[:, k, cs] for k in range(DC)],
                                 [f"slot{sis[o // 4]}"] + [f"rg.{16 + k}.{lt}" for k in range(DC)])
                        evacs.append((b, o))
                    if lt > 0 and hf_ == 0:
                        norm_stats()
                    for (b, o) in evacs:
                        DVE(lambda e: e.tensor_tensor(
                            x_sb[:, o, c0:c1], PS(b), x_sb[:, o, c0:c1], ALU.add),
                            [f"ps{b}", f"x.{o}.{t}"], [f"x.{o}.{t}"])
                    norm_sq(t, hf_ * 4, hf_ * 4 + 4)
                    if lt > 0 and hf_ == 1:
                        norm_scale(l, tiles[lt - 1], lt - 1, V_G2)
            norm_stats()
            norm_scale(l, tiles[-1], ntl - 1, V_G2)

            v5 = lambda sl: sl[:, 0:DC * 2 * 256].rearrange("p (kc s j) -> p kc s j", kc=DC, s=2)
            pend = []
            for f2 in range(FC // 2):
                si = load_slot([(lambda sl: sl[:, 0:DC * 2 * 256], W5r[l, f2])])
                W5 = v5(slots[si])
                sl_ = [f"slot{si}"]
                for fi in range(2):
                    f = 2 * f2 + fi
                    q = f % 2
                    ub, ucb = WB[2 * q], WB[2 * q + 1]
                    ubn, ucn = f"wb{2 * q}", f"wb{2 * q + 1}"
                    b_u = pair_mm(lambda k: W5[:, k, 0, fi * 128:(fi + 1) * 128], hn, lambda k, lt: f"hn.{k}.{lt}", sl_)
                    b_g = pair_mm(lambda k: W5[:, k, 1, fi * 128:(fi + 1) * 128], hn, lambda k, lt: f"hn.{k}.{lt}", sl_)
                    halo_in(gi, ub, 2, car_f[:, f, :], ubn, f"car_f.{f}")
                    ACT(lambda e: e.copy(v2(ub[:, 2:2 + GT]), PSP(b_u)), PSr(b_u), [ubn])
                    POOL(lambda e: e.tensor_copy(car_f[:, f, :], ub[:, n:n + 2]), [ubn, ubn + "h"], [f"car_f.{f}"])
                    if lastg:
                        POOL(lambda e: e.tensor_copy(ost_f[:, f, :], ub[:, n:n + 2 + NS]), [ubn, ubn + "h"], ["ost_f"])
                    conv_taps(n, lastg, ub, ucb, [vecs[:, l, V_FCW + f * 3 + k:V_FCW + f * 3 + k + 1] for k in range(3)],
                              vecs[:, l, V_FCB + f:V_FCB + f + 1],
                              [sf_in[:, f, 0, :], sf_in[:, f, 1, :]],
                              [ubn, ubn + "h", "sf_in"], ucn, wr_extra=[ucn + "h"])

                    def stage2(ucb=ucb, b_g=b_g, f=f, ucn=ucn):
                        ACT(lambda e: e.activation(ucb[:, 0:GT], ucb[:, 0:GT], AF.Silu), [ucn], [ucn, ucn + "h"])
                        DVE(lambda e: e.tensor_tensor(v2(zz[:, f, 0:GT]), v2(ucb[:, 0:GT]), PSP(b_g), ALU.mult),
                            PSr(b_g) + [ucn], [f"rg.{f}.0", f"rg.{f}.1"])
                    if pend:
                        pend.pop(0)()
                    pend.append(stage2)
            while pend:
                pend.pop(0)()
            if lastg:
                out_toks.append(S.dma("sp", lambda e: e.dma_start(out=o_f[l], in_=ost_f[:]), "o_f", reads=["ost_f"]))

            nxt = None
            if gi + 1 < len(GROUPS):
                nxt = (l, gi + 1)
            elif l + 1 < L:
                nxt = (l + 1, 0)
            if nxt is not None:
                for lt, t in enumerate(GROUPS[nxt[1]]):
                    rmsnorm(nxt[0], t, lt, V_G1)

            v6 = lambda sl: sl[:, 0:FC * 128].rearrange("p (kc j) -> p kc j", kc=FC)
            for o in range(DC):
                si = load_slot([(lambda sl: sl[:, 0:FC * 128], W6r[l, o])])
                W6 = v6(slots[si])
                if o == 0:
                    b = next_pair()
                    KS = FC - 2
                    for lt in range(2):
                        mm_group(b + lt, [W6[:, k, :] for k in range(KS)], [zz[:, k, lt * NT:(lt + 1) * NT] for k in range(KS)],
                                 [f"slot{si}"] + [f"rg.{k}.{lt}" for k in range(KS)], last=False)
                    for lt in range(2):
                        mm_group(b + lt, [W6[:, k, :] for k in range(KS, FC)],
                                 [zz[:, k, lt * NT:(lt + 1) * NT] for k in range(KS, FC)],
                                 [f"slot{si}"] + [f"rg.{k}.{lt}" for k in range(KS, FC)], first=False)
                else:
                    b = pair_mm(lambda k: W6[:, k, :], zz, lambda k, lt: f"rg.{k}.{lt}", [f"slot{si}"])
                xo = x_sb[:, o, g0:g0 + GT]
                DVE(lambda e: e.tensor_tensor(v2(xo), PSP(b), v2(xo), ALU.add),
                    PSr(b) + [f"x.{o}.{tiles[0]}", f"x.{o}.{tiles[1]}"], [f"x.{o}.{tiles[0]}", f"x.{o}.{tiles[1]}"])

            if l == L - 1:
                yT_v = yT.rearrange("(c p) n -> p c n", p=128)
                for lt, t in enumerate(tiles):
                    c0, c1 = tile_cols(t)
                    rmsnorm(l, t, lt, None, dst_is_hn=False)
                    c1r = min(c1, NTOK)
                    out_toks.append(S.dma("sp", lambda e: e.dma_start(out=yT_v[:, :, c0:c1r], in_=ystage[:, :, 0:c1r - c0]),
                                          "o_y", reads=[f"ys.{c}" for c in range(DC)] + YA_ALL))

        class _Null:
            def op(self, *a, **k):
                return ("x", 0)

            def dma(self, *a, **k):
                return ("x", 0)
        S_real = S
        plan = {"planning": True, "loads": []}
        state.update({"li": 0, "issued": 0})
        S = _Null()
        saved_bank = state["bank"]
        gen_program()
        plan["planning"] = False
        S = S_real
        state["bank"] = saved_bank
        out_toks.clear()
        gen_program()

        S.final_wait("sp", out_toks)
        S.emit()
    return nc


_NC_CACHE = {}


def _fm(v, nch):
    v = np.asarray(v)
    lead = v.shape[:-1]
    r = v.reshape(lead + (nch, 128))
    nd = r.ndim
    perm = (nd - 1, nd - 2) + tuple(range(nd - 2))
    return np.ascontiguousarray(r.transpose(perm))


def kernel(x_prompt, x_sample, state_conv_a, state_conv_b, state_rglru, state_conv_ffn,
           meta_tokens, norm_mix, norm_ffn, norm_final, w_in, b_gate, conv_a_w, w_a_out,
           conv_b_w, conv_b_b, rg_w_a, rg_b_a, rg_w_x, rg_b_x, rg_lambda, w_b_out, w_o,
           ffn_w_up, ffn_w_gate, ffn_conv_w, ffn_conv_b, ffn_w_down):
    f32 = np.float32
    x_prompt = np.asarray(x_prompt, f32)
    x_sample = np.asarray(x_sample, f32)
    ncores = 8
    vec = np.zeros((128, L, NV), f32)
    for l in range(L):
        vec[:, l, V_G1:V_G1 + 8] = _fm(np.asarray(norm_mix)[l], 8)
        vec[:, l, V_G2:V_G2 + 8] = _fm(np.asarray(norm_ffn)[l], 8)
        vec[:, l, V_BG:V_BG + 16] = _fm(np.asarray(b_gate)[l], 16)
        vec[:, l, V_CAW:V_CAW + 24] = _fm(np.asarray(conv_a_w)[l], 8).reshape(128, 24)
        vec[:, l, V_CBW:V_CBW + 32] = _fm(np.asarray(conv_b_w)[l], 8).reshape(128, 32)
        vec[:, l, V_CBB:V_CBB + 8] = _fm(np.asarray(conv_b_b)[l], 8)
        vec[:, l, V_RBA:V_RBA + 8] = _fm(np.asarray(rg_b_a)[l], 8)
        vec[:, l, V_RBX:V_RBX + 8] = _fm(np.asarray(rg_b_x)[l], 8)
        vec[:, l, V_LAM:V_LAM + 8] = _fm(np.asarray(rg_lambda)[l], 8)
        vec[:, l, V_FCW:V_FCW + 66] = _fm(np.asarray(ffn_conv_w)[l], 22).reshape(128, 66)
        vec[:, l, V_FCB:V_FCB + 22] = _fm(np.asarray(ffn_conv_b)[l], 22)
    gfin = _fm(np.asarray(norm_final), 8)
    rgw = np.zeros((L, 128, 2, 8, 128), f32)
    for wi, W in enumerate((np.asarray(rg_w_a), np.asarray(rg_w_x))):
        for c in range(8):
            for hh in range(2):
                rgw[:, hh * 64:(hh + 1) * 64, wi, c, hh * 64:(hh + 1) * 64] = W[:, 2 * c + hh]
    w_in_ = np.asarray(w_in, f32).reshape(L, DC, 128, 7, DC, 128)
    W1r = np.ascontiguousarray(w_in_[:, :, :, 0:5].transpose(0, 4, 2, 1, 3, 5)).reshape(L, DC, 128, DC * 5 * 128)
    wa_ = np.asarray(w_a_out, f32).reshape(L, DC, 128, DC, 128)
    wb_ = np.asarray(w_b_out, f32).reshape(L, DC, 128, DC, 128)
    W2r = np.empty((L, DC, 128, DC, 4, 128), f32)
    W2r[:, :, :, :, 0] = wa_.transpose(0, 3, 2, 1, 4)
    W2r[:, :, :, :, 1] = wb_.transpose(0, 3, 2, 1, 4)
    W2r[:, :, :, :, 2] = w_in_[:, :, :, 5].transpose(0, 3, 2, 1, 4)
    W2r[:, :, :, :, 3] = w_in_[:, :, :, 6].transpose(0, 3, 2, 1, 4)
    W2r = W2r.reshape(L, DC, 128, DC * 4 * 128)
    wo_ = np.asarray(w_o, f32).reshape(L, DC, 128, 2, 512)
    W3r = np.ascontiguousarray(wo_.transpose(0, 3, 2, 1, 4)).reshape(L, 2, 128, DC * 512)
    wu_ = np.asarray(ffn_w_up, f32).reshape(L, DC, 128, FC // 2, 256)
    wg_ = np.asarray(ffn_w_gate, f32).reshape(L, DC, 128, FC // 2, 256)
    W5r = np.empty((L, FC // 2, 128, DC, 2, 256), f32)
    W5r[:, :, :, :, 0] = wu_.transpose(0, 3, 2, 1, 4)
    W5r[:, :, :, :, 1] = wg_.transpose(0, 3, 2, 1, 4)
    W5r = W5r.reshape(L, FC // 2, 128, DC * 2 * 256)
    wd_ = np.asarray(ffn_w_down, f32).reshape(L, FC, 128, DC, 128)
    W6r = np.ascontiguousarray(wd_.transpose(0, 3, 2, 1, 4)).reshape(L, DC, 128, FC * 128)
    weights = {"W1r": W1r, "W2r": W2r, "W3r": W3r, "W5r": W5r, "W6r": W6r, "vecs": vec, "gfin": gfin, "rgw": rgw}
    sca = np.asarray(state_conv_a, f32)
    scb = np.asarray(state_conv_b, f32)
    srg = np.asarray(state_rglru, f32)
    scf = np.asarray(state_conv_ffn, f32)
    meta = np.asarray(meta_tokens, f32)
    in_maps = []
    for i in range(ncores):
        bs = slice(i * NS, (i + 1) * NS)
        xall = np.concatenate([meta, x_prompt[i], x_sample[bs, 0]], axis=0)
        m = dict(weights)
        m["xT"] = np.ascontiguousarray(xall.T)
        m["sta"] = np.ascontiguousarray(sca[:, bs].reshape(L, NS, 2, 8, 128).transpose(0, 4, 3, 2, 1))
        m["stb"] = np.ascontiguousarray(scb[:, bs].reshape(L, NS, 3, 8, 128).transpose(0, 4, 3, 2, 1))
        m["sth"] = np.ascontiguousarray(srg[:, bs].reshape(L, NS, 8, 128).transpose(0, 3, 2, 1))
        m["stf"] = np.ascontiguousarray(scf[:, bs].reshape(L, NS, 2, 22, 128).transpose(0, 4, 3, 2, 1))
        in_maps.append(m)

    if "nc" not in _NC_CACHE:
        _NC_CACHE["nc"] = build_nc()
    nc = _NC_CACHE["nc"]
    res = run_bass_kernel_spmd(nc, in_maps, core_ids=list(range(ncores)))
    R = res.results

    y_prompt = np.zeros((8, SEQ, D), f32)
    y_sample = np.zeros((128, 1, D), f32)
    p_a = np.zeros((L, 8, 2, D), f32)
    p_b = np.zeros((L, 8, 3, D), f32)
    p_h = np.zeros((L, 8, D), f32)
    p_f = np.zeros((L, 8, 2, DFF), f32)
    s_a = np.zeros((L, 128, 2, D), f32)
    s_b = np.zeros((L, 128, 3, D), f32)
    s_h = np.zeros((L, 128, D), f32)
    s_f = np.zeros((L, 128, 2, DFF), f32)

    def tm(a):
        Lh, P, nch, r = a.shape
        return a.transpose(0, 3, 2, 1).reshape(Lh, r, nch * P)

    for i in range(ncores):
        r = R[i]
        bs = slice(i * NS, (i + 1) * NS)
        yTi = np.asarray(r["yT"])
        y_prompt[i] = yTi[:, NMETA:NMETA + SEQ].T
        y_sample[bs, 0] = yTi[:, NMETA + SEQ:].T
        oa = tm(np.asarray(r["o_a"]))
        ob = tm(np.asarray(r["o_b"]))
        oh = tm(np.asarray(r["o_h"]))
        of = tm(np.asarray(r["o_f"]))
        p_a[:, i] = oa[:, 0:2]
        s_a[:, bs, 1] = oa[:, 2:18]
        p_b[:, i] = ob[:, 0:3]
        s_b[:, bs, 2] = ob[:, 3:19]
        p_h[:, i] = oh[:, 0]
        s_h[:, bs] = oh[:, 1:17]
        p_f[:, i] = of[:, 0:2]
        s_f[:, bs, 1] = of[:, 2:18]
        s_a[:, bs, 0] = tm(np.asarray(r["sh_a"]))
        shb = np.asarray(r["sh_b"])
        for k in range(2):
            s_b[:, bs, k] = tm(np.ascontiguousarray(shb[:, :, :, k, :]))
        s_f[:, bs, 0] = tm(np.asarray(r["sh_f"]))
    return (y_prompt, y_sample, p_a, p_b, p_h, p_f, s_a, s_b, s_h, s_f)
```
